# Optimizing a Trainium2 kernel written in Bass

```python
import jax, jax.numpy as jnp
from jax import lax
import numpy as np

D_MODEL = 1024
BATCH = 8
SEQ = 2048
DEPTH = 2

SG_WIDTH = D_MODEL // 2
SG_CHUNK = 128
SG_GROUPS = 4
CV_WIDTH = D_MODEL // 2
CV_KERNEL = 31
HEAD_DIM = 64
N_Q_HEADS = D_MODEL // 2 // HEAD_DIM
N_KV_HEADS = N_Q_HEADS // 4
Q_WIDTH = N_Q_HEADS * HEAD_DIM
KV_WIDTH = N_KV_HEADS * HEAD_DIM
WINDOW = 128
ROPE_THETA = 10000.0
SC_WIDTH = D_MODEL // 2
SC_KERNEL = 3
N_BRANCH = 4
BRANCH_WIDTH = D_MODEL // 2
D_FF = -(-8 * D_MODEL // (3 * 256)) * 256
EPS = 1e-6

_PROJ_WIDTHS = (2 * SG_WIDTH, 2 * CV_WIDTH, Q_WIDTH, KV_WIDTH, KV_WIDTH, 3 * SC_WIDTH, N_BRANCH * D_MODEL)
PROJ_WIDTH = sum(_PROJ_WIDTHS)
PROJ_SPLITS = tuple(sum(_PROJ_WIDTHS[:i + 1]) for i in range(len(_PROJ_WIDTHS) - 1))

kernel_name = "hybrid_gated_four_mixer_block"


def rmsnorm(x, g):
    xf = x.astype(jnp.float32)
    y = xf * lax.rsqrt(jnp.mean(xf * xf, axis=-1, keepdims=True) + EPS)
    return (y * g.astype(jnp.float32)).astype(x.dtype)


def layernorm(x, g, b):
    xf = x.astype(jnp.float32)
    mu = jnp.mean(xf, axis=-1, keepdims=True)
    var = jnp.mean(jnp.square(xf - mu), axis=-1, keepdims=True)
    y = (xf - mu) * lax.rsqrt(var + EPS)
    return (y * g.astype(jnp.float32) + b.astype(jnp.float32)).astype(x.dtype)


def causal_depthwise_conv(x, w):
    k = w.shape[0]
    return lax.conv_general_dilated(
        x, w[:, None, :].astype(x.dtype), window_strides=(1,), padding=[(k - 1, 0)],
        dimension_numbers=('NWC', 'WIO', 'NWC'), feature_group_count=x.shape[-1])


def rotary(t, cos, sin):
    t1, t2 = jnp.split(t, 2, axis=-1)
    c = cos[None, :, None, :]
    s = sin[None, :, None, :]
    return jnp.concatenate([t1 * c - t2 * s, t2 * c + t1 * s], axis=-1)


def spatial_gating(z, ln_g, ln_b, w_s, b_s):
    u, v = jnp.split(z, 2, axis=-1)
    v = layernorm(v, ln_g, ln_b)
    b, s, _ = v.shape
    nc = s // SG_CHUNK
    vc = v.reshape(b, nc, SG_CHUNK, SG_GROUPS, SG_WIDTH // SG_GROUPS)
    tril = jnp.tril(jnp.ones((SG_CHUNK, SG_CHUNK), dtype=bool))
    w = jnp.where(tril[None], w_s, jnp.zeros((), w_s.dtype))
    mixed = jnp.einsum('gts,bnsgc->bntgc', w, vc) + b_s.T[None, None, :, :, None]
    return u * mixed.reshape(b, s, SG_WIDTH)


def conformer_conv(z, w_dw, b_dw, ln_g, ln_b):
    a, gate = jnp.split(z, 2, axis=-1)
    y = a * jax.nn.sigmoid(gate)
    y = causal_depthwise_conv(y, w_dw) + b_dw
    y = layernorm(y, ln_g, ln_b)
    return jax.nn.silu(y)


def short_gated_conv(z, w_sc):
    bg, cg, h = jnp.split(z, 3, axis=-1)
    return bg * causal_depthwise_conv(cg * h, w_sc)


def sliding_window_attention(q, k, v, sinks):
    b, s, _, _ = q.shape
    nb = s // WINDOW
    g = N_Q_HEADS // N_KV_HEADS
    qb = q.reshape(b, nb, WINDOW, N_KV_HEADS, g, HEAD_DIM)

    def band(t):
        tp = jnp.pad(t, ((0, 0), (WINDOW, 0), (0, 0), (0, 0)))
        prev = tp[:, :s].reshape(b, nb, WINDOW, N_KV_HEADS, HEAD_DIM)
        cur = t.reshape(b, nb, WINDOW, N_KV_HEADS, HEAD_DIM)
        return jnp.concatenate([prev, cur], axis=2)

    kb, vb = band(k), band(v)
    scores = jnp.einsum('bnqhgd,bnkhd->bnhgqk', qb, kb).astype(jnp.float32) * (HEAD_DIM ** -0.5)
    qi = jnp.arange(WINDOW)[None, :, None]
    kj = jnp.arange(2 * WINDOW)[None, None, :]
    blk = jnp.arange(nb)[:, None, None]
    delta = qi + WINDOW - kj
    valid = (delta >= 0) & (delta < WINDOW) & (blk * WINDOW + kj - WINDOW >= 0)
    scores = jnp.where(valid[None, :, None, None], scores, -jnp.inf)
    sink = sinks.astype(jnp.float32).reshape(N_KV_HEADS, g)[None, None, :, :, None, None]
    m = jnp.maximum(jnp.max(scores, axis=-1, keepdims=True), sink)
    p = jnp.exp(scores - m)
    probs = (p / (jnp.sum(p, axis=-1, keepdims=True) + jnp.exp(sink - m))).astype(v.dtype)
    out = jnp.einsum('bnhgqk,bnkhd->bnqhgd', probs, vb)
    return out.reshape(b, s, Q_WIDTH)


def setup_inputs(seed: int = 0) -> dict:
    key = jax.random.key(seed)
    ks = jax.random.split(key, 24)
    f32 = jnp.float32
    nrm = lambda k, shape, scale: jax.random.normal(k, shape, f32) * scale
    gain = lambda k, shape: 1.0 + 0.02 * jax.random.normal(k, shape, f32)
    return {
        "x": nrm(ks[0], (BATCH, SEQ, D_MODEL), 1.0),
        "norm_mix": gain(ks[1], (DEPTH, D_MODEL)),
        "w_in": nrm(ks[2], (DEPTH, D_MODEL, PROJ_WIDTH), D_MODEL ** -0.5),
        "sg_ln_g": gain(ks[3], (DEPTH, SG_WIDTH)),
        "sg_ln_b": nrm(ks[4], (DEPTH, SG_WIDTH), 0.02),
        "sg_w": nrm(ks[5], (DEPTH, SG_GROUPS, SG_CHUNK, SG_CHUNK), SG_CHUNK ** -0.5),
        "sg_b": 1.0 + nrm(ks[6], (DEPTH, SG_GROUPS, SG_CHUNK), 0.1),
        "cv_w": nrm(ks[7], (DEPTH, CV_KERNEL, CV_WIDTH), CV_KERNEL ** -0.5),
        "cv_b": nrm(ks[8], (DEPTH, CV_WIDTH), 0.02),
        "cv_ln_g": gain(ks[9], (DEPTH, CV_WIDTH)),
        "cv_ln_b": nrm(ks[10], (DEPTH, CV_WIDTH), 0.02),
        "attn_sinks": nrm(ks[11], (DEPTH, N_Q_HEADS), 1.0),
        "sc_w": nrm(ks[12], (DEPTH, SC_KERNEL, SC_WIDTH), SC_KERNEL ** -0.5),
        "w_branch": nrm(ks[13], (DEPTH, N_BRANCH, BRANCH_WIDTH, D_MODEL), BRANCH_WIDTH ** -0.5),
        "w_out": nrm(ks[14], (DEPTH, D_MODEL, D_MODEL), 0.5 * D_MODEL ** -0.5),
        "norm_ffn": gain(ks[15], (DEPTH, D_MODEL)),
        "w_gate_up": nrm(ks[16], (DEPTH, D_MODEL, 2 * D_FF), D_MODEL ** -0.5),
        "w_down": nrm(ks[17], (DEPTH, D_FF, D_MODEL), D_FF ** -0.5),
        "norm_final": gain(ks[18], (D_MODEL,)),
    }


def reference(x, norm_mix, w_in, sg_ln_g, sg_ln_b, sg_w, sg_b, cv_w, cv_b, cv_ln_g, cv_ln_b,
              attn_sinks, sc_w, w_branch, w_out, norm_ffn, w_gate_up, w_down, norm_final):
    b, s, _ = x.shape
    pos = jnp.arange(s, dtype=jnp.float32)
    inv_freq = 1.0 / (ROPE_THETA ** (jnp.arange(0, HEAD_DIM, 2, dtype=jnp.float32) / HEAD_DIM))
    ang = pos[:, None] * inv_freq[None, :]
    cos = jnp.cos(ang).astype(x.dtype)
    sin = jnp.sin(ang).astype(x.dtype)

    for l in range(DEPTH):
        xn = rmsnorm(x, norm_mix[l])
        proj = xn @ w_in[l]
        z_a, z_b, q, k, v, z_d, z_g = jnp.split(proj, PROJ_SPLITS, axis=-1)

        y_a = spatial_gating(jax.nn.gelu(z_a, approximate=False), sg_ln_g[l], sg_ln_b[l], sg_w[l], sg_b[l])
        y_b = conformer_conv(z_b, cv_w[l], cv_b[l], cv_ln_g[l], cv_ln_b[l])
        q = rotary(q.reshape(b, s, N_Q_HEADS, HEAD_DIM), cos, sin)
        k = rotary(k.reshape(b, s, N_KV_HEADS, HEAD_DIM), cos, sin)
        v = v.reshape(b, s, N_KV_HEADS, HEAD_DIM)
        y_c = sliding_window_attention(q, k, v, attn_sinks[l])
        y_d = short_gated_conv(z_d, sc_w[l])

        ys = jnp.stack([y_a, y_b, y_c, y_d], axis=0)
        branch = jnp.einsum('nbsc,ncd->nbsd', ys, w_branch[l])
        gates = jax.nn.sigmoid(z_g.reshape(b, s, N_BRANCH, D_MODEL))
        merged = jnp.einsum('bsnd,nbsd->bsd', gates, branch)
        x = x + merged @ w_out[l]

        hn = rmsnorm(x, norm_ffn[l])
        gate, up = jnp.split(hn @ w_gate_up[l], 2, axis=-1)
        x = x + (jax.nn.silu(gate) * up) @ w_down[l]

    return rmsnorm(x, norm_final)
```

```python
import numpy as np
import concourse.bass as bass
import concourse.mybir as mybir

F32 = mybir.dt.float32
BF16 = mybir.dt.bfloat16
AF = mybir.ActivationFunctionType
ALU = mybir.AluOpType
AX = mybir.AxisListType
_DSZ = {F32: 4, BF16: 2}


class View:
    __slots__ = ("ap", "regs")

    def __init__(self, ap, regs):
        self.ap = ap
        self.regs = regs


class Buf:
    def __init__(self, key, base_ap, off_bytes, shape, dtype):
        self.key = key
        self.shape = tuple(shape)
        self.dtype = dtype
        self.esz = _DSZ[dtype]
        self.off = off_bytes
        n = int(np.prod(shape))
        self.nbytes = n * self.esz
        a = base_ap[:, off_bytes // 4:(off_bytes + self.nbytes) // 4]
        if dtype != F32:
            a = a.bitcast(dtype)
        if len(shape) > 1:
            names = " ".join("d%d" % i for i in range(len(shape)))
            kw = {"d%d" % i: int(s) for i, s in enumerate(shape)}
            a = a.rearrange("p (%s) -> p %s" % (names, names), **kw)
        self.ap = a
        st = [1] * len(shape)
        for i in range(len(shape) - 2, -1, -1):
            st[i] = st[i + 1] * shape[i + 1]
        self.strides = st

    def __getitem__(self, key):
        if not isinstance(key, tuple):
            key = (key,)
        pk = key[0]
        fk = list(key[1:]) + [slice(None)] * (len(self.shape) - (len(key) - 1))
        ap = self.ap[(pk,) + tuple(fk)]
        rng = []
        for d, k in enumerate(fk):
            if isinstance(k, int):
                rng.append((k, k + 1))
            else:
                lo = 0 if k.start is None else k.start
                hi = self.shape[d] if k.stop is None else k.stop
                assert k.step in (None, 1)
                assert 0 <= lo < hi <= self.shape[d], (self.key, key, self.shape)
                rng.append((lo, hi))
        nd = len(rng)
        t = nd - 1
        while t > 0 and rng[t] == (0, self.shape[t]):
            t -= 1
        outer = [range(lo, hi) for (lo, hi) in rng[:t]]
        cnt = int(np.prod([len(r) for r in outer])) if outer else 1
        regs = []
        if cnt > 64:
            lo = sum(r[0] * s for r, s in zip(rng, self.strides))
            hi = sum((r[1] - 1) * s for r, s in zip(rng, self.strides)) + 1
            regs.append((self.key, self.off + lo * self.esz, self.off + hi * self.esz))
        else:
            import itertools
            for idx in itertools.product(*outer) if outer else [()]:
                base = sum(i * s for i, s in zip(idx, self.strides[:t]))
                lo = base + rng[t][0] * self.strides[t]
                hi = base + rng[t][1] * self.strides[t]
                regs.append((self.key, self.off + lo * self.esz, self.off + hi * self.esz))
        return View(ap, regs)

    def all(self):
        return self[:]


class Arena:
    def __init__(self, key, base_ap, nbytes):
        self.key = key
        self.base = base_ap
        self.nbytes = nbytes
        self.cur = 0

    def alloc(self, shape, dtype, at=None):
        n = int(np.prod(shape)) * _DSZ[dtype]
        n4 = (n + 3) // 4 * 4
        if at is None:
            at = self.cur
            self.cur += n4
        assert at % 4 == 0
        assert at + n4 <= self.nbytes, ("arena overflow", self.key, at + n4, self.nbytes)
        return Buf(self.key, self.base, at, shape, dtype)


class _Op:
    __slots__ = ("eng", "fn", "waits", "seq", "sig", "dma", "dcnt", "clock")


class Prog:
    ENGS = ("pe", "act", "dve", "pool", "sp")

    def __init__(self, nc):
        self.nc = nc
        self.ops = []
        self.hist = {}
        self.seq = {e: 0 for e in self.ENGS}
        self.last_clock = {e: {} for e in self.ENGS}
        self.dma_cnt = {}
        self.eng_ops = {e: [] for e in self.ENGS}
        self.label = ""
        self.labels = []

    def _deps(self, reads, writes):
        deps = []
        for (key, lo, hi) in reads:
            for r in self.hist.get(key, ()):
                if r[3] and r[0] < hi and lo < r[1]:
                    deps.append((r[2], True))
        for (key, lo, hi) in writes:
            for r in self.hist.get(key, ()):
                if r[0] < hi and lo < r[1]:
                    deps.append((r[2], False))
        return deps

    def _record(self, idx, reads, writes):
        for (key, lo, hi) in reads:
            self.hist.setdefault(key, []).append([lo, hi, idx, False])
        for (key, lo, hi) in writes:
            h = self.hist.setdefault(key, [])
            h[:] = [r for r in h if not (lo <= r[0] and r[1] <= hi)]
            h.append([lo, hi, idx, True])

    def add(self, eng, fn, reads=(), writes=(), dma=None):
        rr = []
        for v in reads:
            rr.extend(v.regs)
        ww = []
        for v in writes:
            ww.extend(v.regs)
        if eng == "pe":
            ww = [(k, lo // 2048 * 2048, -(-hi // 2048) * 2048) if k == "ps" else (k, lo, hi) for (k, lo, hi) in ww]
        op = _Op()
        op.eng = eng
        op.fn = fn
        op.dma = dma
        op.sig = False
        idx = len(self.ops)
        self.seq[eng] += 1
        op.seq = self.seq[eng]
        clock = dict(self.last_clock[eng])
        need = {}
        for (di, is_raw) in self._deps(rr, ww):
            d = self.ops[di]
            if d.dma is not None:
                tk = ("dma", d.dma)
                tv = d.dcnt
            else:
                if d.eng == eng:
                    if (op.seq - d.seq) <= 2 and eng != "pe":
                        tk = ("eng", eng)
                        tv = d.seq
                        if need.get(tk, 0) < tv:
                            need[tk] = tv
                    continue
                tk = ("eng", d.eng)
                tv = d.seq
            if clock.get(tk, 0) >= tv:
                continue
            if need.get(tk, 0) < tv:
                need[tk] = tv
            for k, v in d.clock.items():
                if clock.get(k, 0) < v:
                    clock[k] = v
            clock[tk] = max(clock.get(tk, 0), tv)
        op.waits = need
        self.last_clock[eng] = clock
        if dma is not None:
            self.dma_cnt[dma] = self.dma_cnt.get(dma, 0) + 1
            op.dcnt = self.dma_cnt[dma]
            c2 = dict(clock)
            c2[("eng", eng)] = max(c2.get(("eng", eng), 0), op.seq - 1)
            op.clock = c2
        else:
            op.dcnt = 0
            c2 = dict(clock)
            c2[("eng", eng)] = op.seq
            op.clock = c2
        self.ops.append(op)
        self.labels.append(self.label)
        self.eng_ops[eng].append(idx)
        self._record(idx, rr, ww)
        return idx

    def emit(self, sems, dma_sems):
        nc = self.nc
        engobj = {"pe": nc.tensor, "act": nc.scalar, "dve": nc.vector, "pool": nc.gpsimd, "sp": nc.sync}
        for op in self.ops:
            for (kind, key), v in op.waits.items():
                if kind == "eng":
                    self.ops[self.eng_ops[key][v - 1]].sig = True
        sigcount = {}
        for e in self.ENGS:
            c = 0
            arr = []
            for oi in self.eng_ops[e]:
                if self.ops[oi].sig:
                    c += 1
                arr.append(c)
            sigcount[e] = arr
        nw = 0
        for op in self.ops:
            eo = engobj[op.eng]
            for (kind, key), v in op.waits.items():
                if kind == "eng":
                    eo.wait_ge(sems[key], sigcount[key][v - 1])
                else:
                    eo.wait_ge(dma_sems[key], 16 * v)
                nw += 1
            ins = op.fn()
            if op.dma is not None:
                ins.then_inc(dma_sems[op.dma], 16)
            elif op.sig:
                ins.then_inc(sems[op.eng], 1)
        return nw


def emit_all(P, nc, final_keys=()):
    import contextlib
    with contextlib.ExitStack() as st:
        sems = {e: st.enter_context(nc.semaphore("s_" + e)) for e in P.ENGS}
        dsems = {k: st.enter_context(nc.semaphore("d_" + str(k))) for k in P.dma_cnt}
        nw = P.emit(sems, dsems)
        for k in final_keys:
            nc.sync.wait_ge(dsems[k], 16 * P.dma_cnt[k])
    return nw


import itertools
from concourse.bass_utils import run_bass_kernel_spmd

D = 1024
S = 2048
NP = 512
NPASS = S // NP
DFF = 2816
EPS = 1e-6
PW = 9088

SM_NMIX, SM_NFFN, SM_SGG, SM_SGB, SM_CVB, SM_CVG, SM_CVLB, SM_SINK, SM_SCW, SM_CVW, SM_N = 0, 8, 16, 20, 24, 28, 32, 36, 44, 56, 180
C_ID, C_M01, C_AM, C_NF, C_N = 0, 128, 256, 512, 520


def _win_perm():
    p = []
    p += list(range(512, 1024))
    p += list(range(0, 512))
    for c in range(4):
        p += list(range(1024 + c * 128, 1024 + (c + 1) * 128))
        p += list(range(1536 + c * 128, 1536 + (c + 1) * 128))
    rot = [(d + 32) % 64 for d in range(64)]
    for j in range(4):
        for hq in (j, 4 + j):
            p += [2048 + hq * 64 + d for d in range(64)]
        for hq in (j, 4 + j):
            p += [2048 + hq * 64 + rot[d] for d in range(64)]
    p += list(range(2560, 2688))
    for h in range(2):
        p += [2560 + h * 64 + rot[d] for d in range(64)]
    p += list(range(2688, 2816))
    for c in range(4):
        p += list(range(3328 + c * 128, 3328 + (c + 1) * 128))
        p += list(range(3840 + c * 128, 3840 + (c + 1) * 128))
        p += list(range(2816 + c * 128, 2816 + (c + 1) * 128))
    for j in range(8):
        for n in range(4):
            p += list(range(4352 + n * 1024 + j * 128, 4352 + n * 1024 + (j + 1) * 128))
    assert len(p) == PW
    return np.array(p, dtype=np.int64)


WIN_GROUPS = [512] * 6 + [384] * 5 + [512] * 8
WIN_OFF = [int(x) for x in np.cumsum([0] + WIN_GROUPS[:-1])]


def _host_layout(inp, layers):
    f = np.float32
    perm = _win_perm()
    win = np.stack([np.ascontiguousarray(inp["w_in"][l][:, perm]) for l in layers]).astype(f)
    wbr = np.stack([np.ascontiguousarray(
        inp["w_branch"][l].transpose(1, 0, 2).reshape(512, 4, 8, 128).transpose(2, 0, 1, 3)).reshape(8, 512, 512)
        for l in layers]).astype(f)
    wout = np.stack([inp["w_out"][l] for l in layers]).astype(f)
    gu_perm = []
    for fch in range(22):
        gu_perm += list(range(fch * 128, (fch + 1) * 128))
        gu_perm += list(range(DFF + fch * 128, DFF + (fch + 1) * 128))
    gu_perm = np.array(gu_perm)
    wgu = np.stack([np.ascontiguousarray(inp["w_gate_up"][l][:, gu_perm]) for l in layers]).astype(f)
    wdn = np.stack([inp["w_down"][l] for l in layers]).astype(f)
    small = np.zeros((len(layers), 128, SM_N), f)
    sgw = np.zeros((len(layers), 128, 512), f)
    sgb = np.zeros((len(layers), 128, 512), f)
    for i, l in enumerate(layers):
        small[i, :, SM_NMIX:SM_NMIX + 8] = inp["norm_mix"][l].reshape(8, 128).T
        small[i, :, SM_NFFN:SM_NFFN + 8] = inp["norm_ffn"][l].reshape(8, 128).T
        small[i, :, SM_SGG:SM_SGG + 4] = inp["sg_ln_g"][l].reshape(4, 128).T
        small[i, :, SM_SGB:SM_SGB + 4] = inp["sg_ln_b"][l].reshape(4, 128).T
        small[i, :, SM_CVB:SM_CVB + 4] = inp["cv_b"][l].reshape(4, 128).T
        small[i, :, SM_CVG:SM_CVG + 4] = inp["cv_ln_g"][l].reshape(4, 128).T
        small[i, :, SM_CVLB:SM_CVLB + 4] = inp["cv_ln_b"][l].reshape(4, 128).T
        small[i, :, SM_SINK:SM_SINK + 8] = np.broadcast_to(inp["attn_sinks"][l][None, :], (128, 8))
        small[i, :, SM_SCW:SM_SCW + 12] = inp["sc_w"][l].reshape(3, 4, 128).transpose(2, 1, 0).reshape(128, 12)
        small[i, :, SM_CVW:SM_CVW + 124] = inp["cv_w"][l].reshape(31, 4, 128).transpose(2, 1, 0).reshape(128, 124)
        sgw[i] = inp["sg_w"][l].transpose(2, 0, 1).reshape(128, 512)
        sgb[i] = np.broadcast_to(inp["sg_b"][l].reshape(1, 512), (128, 512))
    consts = np.zeros((128, C_N), f)
    consts[:, C_ID:C_ID + 128] = np.eye(128, dtype=f)
    s_i = np.arange(128)[:, None]
    t_i = np.arange(128)[None, :]
    consts[:, C_M01:C_M01 + 128] = (s_i <= t_i).astype(f)
    am = np.full((128, 256), -30000.0, f)
    qi = np.arange(128)[:, None]
    kj = np.arange(256)[None, :]
    delta = qi + 128 - kj
    am[(delta >= 0) & (delta < 128)] = 0.0
    consts[:, C_AM:C_AM + 256] = am
    consts[:, C_NF:C_NF + 8] = inp["norm_final"].reshape(8, 128).T
    pos = np.arange(S, dtype=f)
    inv_freq = (1.0 / (np.float32(10000.0) ** (np.arange(0, 64, 2, dtype=f) / np.float32(64)))).astype(f)
    ang = (pos[:, None] * inv_freq[None, :]).astype(f)
    cos = np.cos(ang).astype(f)
    sin = np.sin(ang).astype(f)
    d = np.arange(128) % 64
    rope = np.zeros((2, 128, S), f)
    rope[0] = cos[:, d % 32].T
    sgn = np.where(d < 32, -1.0, 1.0).astype(f)
    rope[1] = sin[:, d % 32].T * sgn[:, None]
    return dict(win=win, wbr=wbr, wout=wout, wgu=wgu, wdn=wdn, small=small, sgw=sgw, sgb=sgb,
                consts=consts, rope=rope)


DIAG_ENG = "dve"


def build_program(nl, final_norm, dbg_names=()):
    nc = bass.Bass("TRN2", target_bir_lowering=False)

    def dram(name, shape, kind="ExternalInput"):
        return nc.dram_tensor(name, list(shape), F32, kind=kind).ap()

    x_d = dram("x", [S, D])
    y_d = dram("y", [S, D], "ExternalOutput")
    win_d = dram("win", [nl, D, PW])
    wbr_d = dram("wbr", [nl, 8, 512, 512])
    wout_d = dram("wout", [nl, D, D])
    wgu_d = dram("wgu", [nl, D, 2 * DFF])
    wdn_d = dram("wdn", [nl, DFF, D])
    small_d = dram("small", [nl, 128, SM_N])
    sgw_d = dram("sgw", [nl, 128, 512])
    sgb_d = dram("sgb", [nl, 128, 512])
    consts_d = dram("consts", [128, C_N])
    rope_d = dram("rope", [2, 128, S])
    diag_d = nc.dram_tensor("diag_scratch", [nl, 4, 128, 31 * 128], BF16, kind="Internal").ap()
    dbg_d = {n: nc.dram_tensor("dbg_" + n, [128, sz], (BF16 if bf else F32), kind="ExternalOutput").ap() for n, sz, bf in dbg_names}

    P = Prog(nc)
    E = P.add
    SBN = 206000
    import contextlib
    st = contextlib.ExitStack()
    sb_t = st.enter_context(nc.sbuf_tensor("arena", [128, SBN // 4], F32))
    ps_t = st.enter_context(nc.psum_tensor("psum", [128, 4096], F32))
    SB = Arena("sb", sb_t[:], SBN)
    PS = Arena("ps", ps_t[:], 16384)
    banks = [PS.alloc([512], F32) for _ in range(8)]
    banks_bf = [Buf("ps", ps_t[:], b * 2048, [1024], BF16) for b in range(8)]
    pairs = [Buf("ps", ps_t[:], b * 4096, [1024], F32) for b in range(4)]
    bstate = [0]
    bank_lo = [0]

    def nb():
        b = bstate[0] % 8
        if b < bank_lo[0]:
            bstate[0] += bank_lo[0] - b
            b = bank_lo[0]
        bstate[0] += 1
        return b

    def nb2():
        if bstate[0] % 2:
            bstate[0] += 1
        b = (bstate[0] % 8) // 2
        bstate[0] += 2
        return b

    xT = SB.alloc([8, S], F32)
    cst = SB.alloc([C_N], F32)
    identb = SB.alloc([128], BF16)
    onesb = SB.alloc([128], BF16)
    mhalf = SB.alloc([1], F32)
    amb = SB.alloc([256], BF16)
    small = SB.alloc([SM_N], F32)
    nsink = SB.alloc([8], F32)
    wtril = SB.alloc([4, 128], BF16)
    T1 = SB.alloc([4, 128], F32)
    cosT = SB.alloc([NP], F32)
    sinT = SB.alloc([NP], F32)
    xnT = SB.alloc([8, NP], BF16)
    rstd = SB.alloc([NP], F32)
    rstd2 = SB.alloc([NP], F32)
    wslots = [SB.alloc([8, 512], BF16) for _ in range(3)]
    yglu = SB.alloc([4, 30 + NP], BF16)
    kT = [SB.alloc([128 + NP], BF16) for _ in range(2)]
    Vpad = SB.alloc([5, 2, 2, 128], BF16)
    prod = SB.alloc([4, 2 + NP], F32)
    X0 = SB.cur
    yT = SB.alloc([4, 4, NP], BF16)
    A0 = SB.cur
    diag = SB.alloc([2, 31, 128], BF16)
    ybf = SB.alloc([4, NP], BF16)
    ysq = SB.alloc([4, NP], BF16)
    sig = SB.alloc([2, NP], BF16)
    cmean = SB.alloc([NP], F32)
    crstd = SB.alloc([NP], F32)
    ctmp = SB.alloc([2, NP], F32)
    A1 = SB.cur
    SB.cur = A0
    cgs = SB.alloc([NP], F32)
    acc = SB.alloc([NP], F32)
    bgs = SB.alloc([2, NP], BF16)
    mergedT = SB.alloc([8, NP], BF16)
    sg = SB.alloc([4, NP], BF16)
    wbrs = [SB.alloc([4, 4, 128], BF16) for _ in range(2)]
    macc = SB.alloc([NP], F32)
    mtmp = SB.alloc([2, NP], F32)
    A2 = SB.cur
    SB.cur = max(A1, A2)
    B0 = SB.cur
    uT = SB.alloc([4, NP], BF16)
    vhat = SB.alloc([4, NP], BF16)
    vgt = SB.alloc([2, NP], F32)
    gst = SB.alloc([64], F32)
    gtmp = SB.alloc([NP], F32)
    B1 = SB.cur
    SB.cur = B0
    qT = SB.alloc([4, NP], BF16)
    rt1 = SB.alloc([NP], F32)
    rt2 = SB.alloc([NP], F32)
    pu = SB.alloc([2, 4, 256], BF16)
    pb = SB.alloc([2, 4, 256], BF16)
    pTs = SB.alloc([2, 1024], BF16)
    ast = SB.alloc([2, 32], F32)
    B2 = SB.cur
    SB.cur = B0
    sq = SB.alloc([8, NP], BF16)
    B3 = SB.cur
    SB.cur = B0
    sgw32 = SB.alloc([512], F32)
    sgb32 = SB.alloc([512], F32)
    B4 = SB.cur
    SB.cur = max(B1, B2, B3, B4)
    X1 = SB.cur
    SB.cur = X0
    gT = SB.alloc([22, NP], BF16)
    wds = [SB.alloc([22, 256], BF16) for _ in range(2)]
    sl = SB.alloc([2, NP], F32)
    F1 = SB.cur
    SB.cur = X0
    xin = SB.alloc([4, D], F32)
    xo = SB.alloc([2, 8, 128], F32)
    yout = SB.alloc([2, D], F32)
    F2 = SB.cur
    SB.cur = max(X1, F1, F2)
    assert SB.cur <= SBN, SB.cur
    print("SBUF bytes used per partition:", SB.cur)

    ident = Buf("sb", sb_t[:], cst.off + C_ID * 4, [128], F32)
    m01 = Buf("sb", sb_t[:], cst.off + C_M01 * 4, [128], F32)
    amask = Buf("sb", sb_t[:], cst.off + C_AM * 4, [256], F32)
    nfin = Buf("sb", sb_t[:], cst.off + C_NF * 4, [8], F32)

    def smv(off, n=1):
        return small[:, off:off + n]

    def bc(v, shape):
        return View(v.ap.to_broadcast(list(shape)), v.regs)

    def mm(out_v, lhsT_v, rhs_v, start, stop):
        E("pe", lambda: nc.tensor.matmul(out_v.ap, lhsT=lhsT_v.ap, rhs=rhs_v.ap, start=start, stop=stop),
          reads=[lhsT_v, rhs_v], writes=[out_v])

    def tr(out_v, in_v, id_v):
        E("pe", lambda: nc.tensor.transpose(out=out_v.ap, in_=in_v.ap, identity=id_v.ap),
          reads=[in_v, id_v], writes=[out_v])

    def act(out_v, in_v, func, bias=None, scale=None, accum=None):
        rd = [in_v]
        kw = {}
        if bias is not None:
            if isinstance(bias, View):
                rd.append(bias)
                kw["bias"] = bias.ap
            else:
                kw["bias"] = float(bias)
        if scale is not None:
            if isinstance(scale, View):
                rd.append(scale)
                kw["scale"] = scale.ap
            else:
                kw["scale"] = float(scale)
        wr = [out_v]
        if accum is not None:
            wr.append(accum)
            kw["accum_out"] = accum.ap
        E("act", lambda: nc.scalar.activation(out=out_v.ap, in_=in_v.ap, func=func, **kw), reads=rd, writes=wr)

    def tt(out_v, a_v, b_v, op, eng="dve"):
        eo = nc.vector if eng == "dve" else nc.gpsimd
        E(eng, lambda: eo.tensor_tensor(out=out_v.ap, in0=a_v.ap, in1=b_v.ap, op=op), reads=[a_v, b_v], writes=[out_v])

    def ts(out_v, in_v, s1, op0, s2=None, op1=None, eng="dve"):
        eo = nc.vector if eng == "dve" else nc.gpsimd
        rd = [in_v]
        a1 = s1
        a2 = s2
        if isinstance(s1, View):
            rd.append(s1)
            a1 = s1.ap
        if isinstance(s2, View):
            rd.append(s2)
            a2 = s2.ap
        if op1 is None:
            E(eng, lambda: eo.tensor_scalar(out=out_v.ap, in0=in_v.ap, scalar1=a1, scalar2=None, op0=op0), reads=rd, writes=[out_v])
        else:
            E(eng, lambda: eo.tensor_scalar(out=out_v.ap, in0=in_v.ap, scalar1=a1, scalar2=a2, op0=op0, op1=op1), reads=rd, writes=[out_v])

    def stt(out_v, in0_v, sc, in1_v, op0, op1):
        rd = [in0_v, in1_v]
        a = sc
        if isinstance(sc, View):
            rd.append(sc)
            a = sc.ap
        E("dve", lambda: nc.vector.scalar_tensor_tensor(out=out_v.ap, in0=in0_v.ap, scalar=a, in1=in1_v.ap, op0=op0, op1=op1),
          reads=rd, writes=[out_v])

    def cp(eng, out_v, in_v):
        if eng == "act":
            E("act", lambda: nc.scalar.copy(out=out_v.ap, in_=in_v.ap), reads=[in_v], writes=[out_v])
        elif eng == "dve":
            E("dve", lambda: nc.vector.tensor_copy(out=out_v.ap, in_=in_v.ap), reads=[in_v], writes=[out_v])
        else:
            E("pool", lambda: nc.gpsimd.tensor_copy(out=out_v.ap, in_=in_v.ap), reads=[in_v], writes=[out_v])

    def mset(eng, out_v, val):
        eo = {"dve": nc.vector, "pool": nc.gpsimd}[eng]
        E(eng, lambda: eo.memset(out_v.ap, val), writes=[out_v])

    def rsqrt_pool(out_v, in_v):
        n = in_v.ap.shape
        E("pool", lambda: nc.gpsimd.tensor_tensor(out=out_v.ap, in0=in_v.ap, in1=mhalf[:, 0:1].ap.to_broadcast(list(n)), op=ALU.pow),
          reads=[in_v, mhalf.all()], writes=[out_v])

    def rsqrt_big(out_v, in_v):
        act(out_v, in_v, AF.Sqrt)
        E("dve", lambda: nc.vector.reciprocal(out=out_v.ap, in_=out_v.ap), reads=[out_v], writes=[out_v])

    def dma_sp(out_v, in_ap, key, src_views=()):
        E("sp", lambda: nc.sync.dma_start(out=out_v.ap, in_=in_ap), reads=list(src_views), writes=[out_v], dma=key)

    def dma_out(out_ap, in_v, key):
        E("sp", lambda: nc.sync.dma_start(out=out_ap, in_=in_v.ap), reads=[in_v], dma=key)

    def dma_w(out_v, in_ap, key):
        E("pool", lambda: nc.gpsimd.dma_start(out=out_v.ap, in_=in_ap), writes=[out_v], dma=key)

    def dump(name, view, rearr=None, **kw):
        if name in dbg_d:
            dst = dbg_d[name]
            if rearr:
                dst = dst.rearrange(rearr, **kw)
            dma_out(dst, view, "dbg_" + name)

    sched = []
    for l in range(nl):
        for p in range(NPASS):
            for g in (0, 1, 4, 5, 6, 2, 3, 7, 8, 9, 10):
                sched.append(("win", l, g))
            for j in range(8):
                sched.append(("win", l, 11 + j))
                sched.append(("wbr", l, j))
            for g in range(2):
                sched.append(("wout", l, g))
            for g in range(11):
                sched.append(("wgu", l, g))
            for g in range(4):
                sched.append(("wdn", l, g))
    loaded = {}

    w_use = [0]
    _cls = lambda kk: "w512" if kk in ("win", "wout", "wgu") else kk
    cls_index = []
    _cc = {"w512": 0, "wbr": 0, "wdn": 0}
    pending = {"w512": [], "wbr": [], "wdn": []}
    for si, (kk, _l, _g) in enumerate(sched):
        cls_index.append(_cc[_cls(kk)])
        _cc[_cls(kk)] += 1
        pending[_cls(kk)].append(si)
    retired = {"w512": 0, "wbr": 0, "wdn": 0}
    nslots = {"w512": 3, "wbr": 2, "wdn": 2}
    look = {"w512": 1000, "wbr": 4, "wdn": 5}
    issued = set()
    gate_ok = {"wbr": True}

    def issue_idx(i):
        kind, l, g = sched[i]
        if kind in ("win", "wout", "wgu"):
            s = cls_index[i] % 3
            slot = wslots[s]
            if kind == "win":
                w = WIN_GROUPS[g]
                src = win_d[l][:, WIN_OFF[g]:WIN_OFF[g] + w]
            elif kind == "wout":
                w = 512
                src = wout_d[l][:, g * 512:(g + 1) * 512]
            else:
                w = 512
                src = wgu_d[l][:, g * 512:(g + 1) * 512]
            dma_w(slot[:, :, 0:w], src.rearrange("(kc p) n -> p kc n", p=128), "w512_%d" % s)
        elif kind == "wbr":
            s = cls_index[i] % 2
            slot = wbrs[s]
            dma_w(View(slot.ap.rearrange("p kc n d -> p kc (n d)"), slot.all().regs),
                  wbr_d[l][g].rearrange("(kc p) n -> p kc n", p=128), "wbr_%d" % s)
        else:
            s = cls_index[i] % 2
            slot = wds[s]
            dma_w(slot.all(), wdn_d[l][:, g * 256:(g + 1) * 256].rearrange("(kc p) n -> p kc n", p=128), "wdn_%d" % s)
        loaded[i] = slot
        issued.add(i)

    def next_w(kind, l, g):
        i = w_use[0]
        assert sched[i] == (kind, l, g), (sched[i], kind, l, g)
        c = _cls(kind)
        retired[c] = cls_index[i]
        progress = True
        while progress:
            progress = False
            heads = [(pending[cc][0], cc) for cc in pending if pending[cc]]
            for j, cc in sorted(heads):
                if cls_index[j] < retired[cc] + nslots[cc] and j <= i + look[cc] and gate_ok.get(cc, True):
                    issue_idx(j)
                    pending[cc].pop(0)
                    progress = True
                    break
        assert i in issued
        w_use[0] += 1
        return loaded.pop(i)

    dma_sp(cst.all(), consts_d, "c0")
    cp("dve", identb.all(), ident.all())
    cp("dve", amb.all(), amask.all())
    mset("dve", onesb.all(), 1.0)
    mset("dve", mhalf.all(), -0.5)
    mset("dve", Vpad.all(), 0.0)
    mset("dve", kT[0].all(), 0.0)
    mset("dve", kT[1].all(), 0.0)

    for ti in range(16):
        r = ti % 4
        dma_sp(xin[:, r, :], x_d[ti * 128:(ti + 1) * 128, :], "xin%d" % r)
        for half in range(2):
            b = nb()
            for q in range(4):
                kc = half * 4 + q
                tr(banks[b][:, q * 128:(q + 1) * 128], xin[:, r, kc * 128:(kc + 1) * 128], ident.all())
            src = View(banks[b].ap.rearrange("p (a b) -> p a b", a=4), banks[b].all().regs)
            dst = xT[:, half * 4:(half + 1) * 4, ti * 128:(ti + 1) * 128]
            cp("act" if half == 0 else "dve", dst, src)

    def rms_stats(t0, rs, split=False):
        for kc in range(8):
            if kc % 2 == 0 or not split:
                act(sq[:, kc, :], xT[:, kc, t0:t0 + NP], AF.Square)
            else:
                tt(sq[:, kc, :], xT[:, kc, t0:t0 + NP], xT[:, kc, t0:t0 + NP], ALU.mult)
        b = nb()
        for kc in range(8):
            mm(banks[b].all(), onesb.all(), sq[:, kc, :], kc == 0, kc == 7)
        ts(rs.all(), banks[b].all(), 1.0 / D, ALU.mult, EPS, ALU.add)
        rsqrt_big(rs.all(), rs.all())

    def rms_apply(t0, gcol, out_bf, rs):
        for kc in range(8):
            stt(out_bf[:, kc, :], xT[:, kc, t0:t0 + NP], smv(gcol + kc), rs.all(), ALU.mult, ALU.mult)

    def rmsnorm(t0, gcol, out_bf):
        rms_stats(t0, rstd)
        rms_apply(t0, gcol, out_bf, rstd)

    def proj_fm(slot, c, rhsT, nk=8):
        b = nb()
        for kc in range(nk):
            mm(banks[b].all(), slot[:, kc, c * 128:(c + 1) * 128], rhsT[:, kc, :], kc == 0, kc == nk - 1)
        return b

    hoisted = [False]
    for l in range(nl):
        dma_sp(small.all(), small_d[l], "small")
        dma_sp(sgw32.all(), sgw_d[l], "sgw")
        dma_sp(sgb32.all(), sgb_d[l], "sgb")
        ts(nsink.all(), smv(SM_SINK, 8), -1.0, ALU.mult)
        wtv = View(wtril.ap.rearrange("p g t -> p (g t)"), wtril.all().regs)
        sg4 = View(sgw32.ap.rearrange("p (g t) -> p g t", g=4), sgw32.all().regs)
        tt(wtril.all(), sg4, bc(View(m01.ap.rearrange("p (o t) -> p o t", o=1), m01.all().regs), [128, 4, 128]), ALU.mult)
        b = nb()
        mm(banks[b].all(), onesb.all(), wtv, True, True)
        for g in range(4):
            stt(T1[:, g, :], banks[b][:, g * 128:(g + 1) * 128], smv(SM_SGB + g), sgb32[:, g * 128:(g + 1) * 128], ALU.mult, ALU.add)

        for c in range(4):
            idb = View(identb.ap.rearrange("p (o t) -> p o t", o=1), identb.all().regs)
            cw = small[:, SM_CVW + c * 31:SM_CVW + (c + 1) * 31]
            cw3 = View(cw.ap.rearrange("p (k o) -> p k o", o=1), cw.regs)
            tt(diag[:, c % 2], bc(idb, [128, 31, 128]), bc(cw3, [128, 31, 128]), ALU.mult)
            dgv = View(diag[:, c % 2].ap.rearrange("p k t -> p (k t)"), diag[:, c % 2].regs)
            E("sp", (lambda l=l, c=c, dgv=dgv: nc.sync.dma_start(out=diag_d[l][c], in_=dgv.ap)),
              reads=[dgv], writes=[View(None, [("dg", (l * 4 + c) * 100, (l * 4 + c) * 100 + 100)])], dma="dgw%d" % (c % 2))

        for p in range(NPASS):
            t0 = p * NP
            first = (p == 0)
            dma_sp(cosT.all(), rope_d[0][:, t0:t0 + NP], "cos")
            dma_sp(sinT.all(), rope_d[1][:, t0:t0 + NP], "sin")
            if first:
                mset("dve", yglu[:, :, 0:30], 0.0)
                mset("dve", prod[:, :, 0:2], 0.0)
            else:
                cp("dve", yglu[:, :, 0:30], yglu[:, :, NP:NP + 30])
                cp("dve", prod[:, :, 0:2], prod[:, :, NP:NP + 2])
                cp("dve", kT[0][:, 0:128], kT[0][:, NP:NP + 128])
                cp("dve", kT[1][:, 0:128], kT[1][:, NP:NP + 128])
                cp("dve", Vpad[:, 0], Vpad[:, 4])

            P.label = "L%dP%d norm1" % (l, p)
            gate_ok["wbr"] = False
            if hoisted[0]:
                rms_apply(t0, SM_NMIX, xnT, rstd2)
            else:
                rmsnorm(t0, SM_NMIX, xnT)
            if l == 0 and p == 0:
                dump("xn", xnT.all())

            P.label = "L%dP%d gmlp" % (l, p)
            slot = next_w("win", l, 0)
            for ti in range(4):
                b = nb()
                for kc in range(8):
                    mm(banks[b].all(), xnT[:, kc, ti * 128:(ti + 1) * 128], slot[:, kc, :], kc == 0, kc == 7)
                r = ti % 2
                act(vgt[:, r, :], banks[b].all(), AF.Gelu)
                E("dve", (lambda r=r, ti=ti: nc.vector.bn_stats(out=gst[:, ti * 8:ti * 8 + 6].ap, in_=vgt[:, r, :].ap)),
                  reads=[vgt[:, r, :]], writes=[gst[:, ti * 8:ti * 8 + 6]])
                E("dve", (lambda ti=ti: nc.vector.bn_aggr(out=gst[:, 32 + ti * 4:32 + ti * 4 + 2].ap, in_=gst[:, ti * 8:ti * 8 + 6].ap)),
                  reads=[gst[:, ti * 8:ti * 8 + 6]], writes=[gst[:, 32 + ti * 4:32 + ti * 4 + 2]])
                ts(gst[:, 32 + ti * 4 + 2:32 + ti * 4 + 3], gst[:, 32 + ti * 4 + 1:32 + ti * 4 + 2], EPS, ALU.add)
                rsqrt_pool(gst[:, 32 + ti * 4 + 3:32 + ti * 4 + 4], gst[:, 32 + ti * 4 + 2:32 + ti * 4 + 3])
                ts(vhat[:, ti, :], vgt[:, r, :], gst[:, 32 + ti * 4:32 + ti * 4 + 1], ALU.subtract,
                   gst[:, 32 + ti * 4 + 3:32 + ti * 4 + 4], ALU.mult)
            slot = next_w("win", l, 1)
            for c in range(4):
                b = proj_fm(slot, c, xnT)
                act(uT[:, c, :], banks[b].all(), AF.Gelu)
            for g in range(4):
                b = nb()
                for nch in range(4):
                    mm(banks[b][:, nch * 128:(nch + 1) * 128], vhat[:, nch, g * 128:(g + 1) * 128], wtril[:, g, :], True, True)
                src = View(banks[b].ap.rearrange("p (a b) -> p a b", a=4), banks[b].all().regs)
                t1b = bc(View(T1[:, g:g + 1, :].ap, T1[:, g, :].regs), [128, 4, 128])
                gt3 = View(gtmp.ap.rearrange("p (a b) -> p a b", a=4), gtmp.all().regs)
                E("dve", (lambda src=src, t1b=t1b, gt3=gt3, g=g: nc.vector.scalar_tensor_tensor(
                    out=gt3.ap, in0=src.ap, scalar=smv(SM_SGG + g).ap, in1=t1b.ap, op0=ALU.mult, op1=ALU.add)),
                  reads=[src, t1b, smv(SM_SGG + g)], writes=[gt3])
                tt(yT[:, 0, g, :], gtmp.all(), uT[:, g, :], ALU.mult)

            def build_diag(c):
                dgv = View(diag[:, c % 2].ap.rearrange("p k t -> p (k t)"), diag[:, c % 2].regs)
                E("sp", (lambda c=c, dgv=dgv, l=l: nc.sync.dma_start(out=dgv.ap, in_=diag_d[l][c])),
                  reads=[View(None, [("dg", (l * 4 + c) * 100, (l * 4 + c) * 100 + 100)])], writes=[dgv], dma="dgr%d" % (c % 2))

            build_diag(0)
            build_diag(1)
            P.label = "L%dP%d attn" % (l, p)
            for gi in range(2):
                slot = next_w("win", l, 4 + gi)
                for ji in range(2):
                    j = gi * 2 + ji
                    bq = proj_fm(slot, 2 * ji, xnT)
                    bp = proj_fm(slot, 2 * ji + 1, xnT)
                    tt(rt1.all(), banks[bq].all(), cosT.all(), ALU.mult)
                    tt(rt2.all(), banks[bp].all(), sinT.all(), ALU.mult)
                    tt(qT[:, j, :], rt1.all(), rt2.all(), ALU.add)
            slot = next_w("win", l, 6)
            bk = proj_fm(slot, 0, xnT)
            bkp = proj_fm(slot, 1, xnT)
            tt(rt1.all(), banks[bk].all(), cosT.all(), ALU.mult)
            tt(rt2.all(), banks[bkp].all(), sinT.all(), ALU.mult)
            tt(kT[0][0:64, 128:128 + NP], rt1[0:64, :], rt2[0:64, :], ALU.add)
            tt(kT[1][64:128, 128:128 + NP], rt1[64:128, :], rt2[64:128, :], ALU.add)
            bv = nb()
            for ti in range(4):
                for kc in range(8):
                    mm(banks[bv][:, ti * 128:(ti + 1) * 128], xnT[:, kc, ti * 128:(ti + 1) * 128], slot[:, kc, 256:384], kc == 0, kc == 7)
            for ti in range(4):
                srcv = View(banks[bv][:, ti * 128:(ti + 1) * 128].ap.rearrange("p (h d) -> p h d", h=2), banks[bv][:, ti * 128:(ti + 1) * 128].regs)
                cp("act", Vpad[:, 1 + ti, :, 0, 0:64], srcv)
                cp("act", Vpad[:, 1 + ti, :, 1, 64:128], srcv)
            def conf_units():
                for gi in range(2):
                    slot = next_w("win", l, 2 + gi)
                    for ci in range(2):
                        c = gi * 2 + ci
                        ba = proj_fm(slot, 2 * ci, xnT)
                        bg_ = proj_fm(slot, 2 * ci + 1, xnT)
                        r = c % 2
                        act(sig[:, r, :], banks[bg_].all(), AF.Sigmoid)
                        tt(yglu[:, c, 30:30 + NP], banks[ba].all(), sig[:, r, :], ALU.mult)
                        yield
                for c in range(4):
                    r = c % 2
                    b = nb()
                    for k in range(31):
                        mm(banks[b].all(), diag[:, r, k, :], yglu[:, c, k:k + NP], k == 0, k == 30)
                    if c + 2 < 4:
                        build_diag(c + 2)
                    act(ybf[:, c, :], banks[b].all(), AF.Identity, bias=smv(SM_CVB + c))
                    act(ysq[:, c, :], banks[b].all(), AF.Square, bias=smv(SM_CVB + c))
                    yield
                b1 = nb()
                for c in range(4):
                    mm(banks[b1].all(), onesb.all(), ybf[:, c, :], c == 0, c == 3)
                b2 = nb()
                for c in range(4):
                    mm(banks[b2].all(), onesb.all(), ysq[:, c, :], c == 0, c == 3)
                ts(cmean.all(), banks[b1].all(), 1.0 / 512, ALU.mult)
                tt(ctmp[:, 0, :], cmean.all(), cmean.all(), ALU.mult)
                stt(crstd.all(), banks[b2].all(), 1.0 / 512, ctmp[:, 0, :], ALU.mult, ALU.subtract)
                ts(crstd.all(), crstd.all(), EPS, ALU.add)
                yield

            def conf_tail_rsqrt():
                rsqrt_big(crstd.all(), crstd.all())

            def conf_tail(cs):
                for c in cs:
                    r = c % 2
                    tt(ctmp[:, r, :], ybf[:, c, :], cmean.all(), ALU.subtract)
                    tt(ctmp[:, r, :], ctmp[:, r, :], crstd.all(), ALU.mult)
                    act(yT[:, 1, c, :], ctmp[:, r, :], AF.Silu, bias=smv(SM_CVLB + c), scale=smv(SM_CVG + c))


            def sconv_units():
                for c in range(4):
                    slot = next_w("win", l, 7 + c)
                    bcg = proj_fm(slot, 0, xnT)
                    bh = proj_fm(slot, 1, xnT)
                    bbg = proj_fm(slot, 2, xnT)
                    r = c % 2
                    cp("act", cgs.all(), banks[bcg].all())
                    tt(prod[:, c, 2:2 + NP], banks[bh].all(), cgs.all(), ALU.mult)
                    cp("act", bgs[:, r, :], banks[bbg].all())
                    ts(acc.all(), prod[:, c, 2:2 + NP], smv(SM_SCW + c * 3 + 2), ALU.mult)
                    stt(acc.all(), prod[:, c, 1:1 + NP], smv(SM_SCW + c * 3 + 1), acc.all(), ALU.mult, ALU.add)
                    stt(acc.all(), prod[:, c, 0:NP], smv(SM_SCW + c * 3 + 0), acc.all(), ALU.mult, ALU.add)
                    tt(yT[:, 3, c, :], acc.all(), bgs[:, r, :], ALU.mult)
                    yield

            tstate = {}

            def att_A(t):
                n, h = t // 2, t % 2
                seq_first = (first and n == 0)
                nkb = 1 if seq_first else 2
                nk = nkb * 128
                k0 = 128 if seq_first else n * 128
                amv = amb[:, 256 - nk:256]
                sc = pairs[h]
                for g in range(4):
                    mm(sc[:, g * nk:(g + 1) * nk], qT[:, g, n * 128:(n + 1) * 128], kT[h][:, k0:k0 + nk], True, False)
                    mm(sc[:, g * nk:(g + 1) * nk], identb.all(), amv, False, True)
                sc3 = View(sc[:, 0:4 * nk].ap.rearrange("p (g k) -> p g k", g=4), sc[:, 0:4 * nk].regs)
                E("dve", (lambda: nc.vector.tensor_reduce(out=ast[:, h, 0:4].ap, in_=sc3.ap, axis=AX.X, op=ALU.max)),
                  reads=[sc3], writes=[ast[:, h, 0:4]])
                stt(ast[:, h, 4:8], ast[:, h, 0:4], -0.125, nsink[:, h * 4:(h + 1) * 4], ALU.mult, ALU.min)
                tt(ast[:, h, 12:16], ast[:, h, 4:8], smv(SM_SINK + h * 4, 4), ALU.add)
                pu3 = View(pu[:, h].ap.rearrange("p g k -> p (g k)")[:, 0:4 * nk].rearrange("p (g k) -> p g k", g=4), pu[:, h].regs)
                for g in range(4):
                    act(View(pu3.ap[:, g, :], pu[:, h].regs), View(sc3.ap[:, g, :], sc3.regs), AF.Exp,
                        bias=ast[:, h, 4 + g:5 + g], scale=0.125, accum=ast[:, h, 8 + g:9 + g])
                act(ast[:, h, 12:16], ast[:, h, 12:16], AF.Exp)
                tt(ast[:, h, 16:20], ast[:, h, 8:12], ast[:, h, 12:16], ALU.add)
                E("dve", (lambda: nc.vector.reciprocal(out=ast[:, h, 20:24].ap, in_=ast[:, h, 16:20].ap)),
                  reads=[ast[:, h, 16:20]], writes=[ast[:, h, 20:24]])
                pb3 = View(pb[:, h].ap.rearrange("p g k -> p (g k)")[:, 0:4 * nk].rearrange("p (g k) -> p g k", g=4), pb[:, h].regs)
                rd3 = View(ast[:, h, 20:24].ap.rearrange("p (g o) -> p g o", o=1).to_broadcast([128, 4, nk]), ast[:, h, 20:24].regs)
                tt(pb3, pu3, rd3, ALU.mult)
                tstate[t] = (n, h, seq_first, nkb, nk, pb3)

            def att_B(t):
                n, h, seq_first, nkb, nk, pb3 = tstate[t]
                bt = nb()
                ptb = banks_bf[bt]
                for g in range(4):
                    for kb in range(nkb):
                        tr(ptb[:, (g * nkb + kb) * 128:(g * nkb + kb + 1) * 128],
                           View(pb3.ap[:, g, kb * 128:(kb + 1) * 128], pb[:, h].regs), identb.all())
                cp("act", pTs[:, h, 0:4 * nk], ptb[:, 0:4 * nk])

            def att_C(t):
                n, h, seq_first, nkb, nk, pb3 = tstate[t]
                bo = nb()
                for prr in range(2):
                    cnt = 0
                    tot = 2 * nkb
                    for gi2 in range(2):
                        g = prr * 2 + gi2
                        for kb in range(nkb):
                            vt = (n + kb) if not seq_first else 1
                            mm(banks[bo][:, prr * 128:(prr + 1) * 128], Vpad[:, vt, h, gi2, :],
                               pTs[:, h, (g * nkb + kb) * 128:(g * nkb + kb + 1) * 128], cnt == 0, cnt == tot - 1)
                            cnt += 1
                srcv = View(banks[bo][:, 0:256].ap.rearrange("p (a b) -> p a b", a=2), banks[bo][:, 0:256].regs)
                cp("act", yT[:, 2, h * 2:h * 2 + 2, n * 128:(n + 1) * 128], srcv)

            bank_lo[0] = 4
            og = itertools.chain(conf_units(), sconv_units())

            def other(k=1):
                for _ in range(k):
                    P.label = "L%dP%d conf" % (l, p)
                    try:
                        next(og)
                    except StopIteration:
                        pass
                P.label = "L%dP%d attn" % (l, p)

            P.label = "L%dP%d attn" % (l, p)
            att_A(0)
            other(1)
            att_A(1)
            other(1)
            for t in range(8):
                att_B(t)
                other(1)
                att_C(t)
                if t + 2 < 8:
                    att_A(t + 2)
                if t <= 3:
                    other(1)
                if t == 5:
                    P.label = "L%dP%d conf" % (l, p)
                    conf_tail_rsqrt()
                    P.label = "L%dP%d attn" % (l, p)
                if t == 6:
                    P.label = "L%dP%d conf" % (l, p)
                    conf_tail((0, 1))
                    P.label = "L%dP%d attn" % (l, p)
            other(40)
            P.label = "L%dP%d conf" % (l, p)
            conf_tail((2, 3))
            bank_lo[0] = 0

            if l == 0 and p == 0:
                dump("yT", View(yT.ap.rearrange("p a b c -> p (a b c)"), yT.all().regs))

            P.label = "L%dP%d merge" % (l, p)
            gate_ok["wbr"] = True
            for j in range(8):
                slot = next_w("win", l, 11 + j)
                wb = next_w("wbr", l, j)
                for ni, n in enumerate((0, 3, 2, 1)):
                    bgt = proj_fm(slot, n, xnT)
                    bb = nb()
                    for kc in range(4):
                        mm(banks[bb].all(), wb[:, kc, n, :], yT[:, n, kc, :], kc == 0, kc == 3)
                    act(sg[:, n, :], banks[bgt].all(), AF.Sigmoid)
                    if ni == 0:
                        tt(macc.all(), banks[bb].all(), sg[:, n, :], ALU.mult)
                    else:
                        r = ni % 2
                        tt(mtmp[:, r, :], banks[bb].all(), sg[:, n, :], ALU.mult)
                        if ni < 3:
                            tt(macc.all(), macc.all(), mtmp[:, r, :], ALU.add)
                        else:
                            tt(mergedT[:, j, :], macc.all(), mtmp[:, r, :], ALU.add)
            P.label = "L%dP%d wout" % (l, p)
            for g in range(2):
                slot = next_w("wout", l, g)
                for jj in range(4):
                    j = g * 4 + jj
                    b = proj_fm(slot, jj, mergedT)
                    tt(xT[:, j, t0:t0 + NP], xT[:, j, t0:t0 + NP], banks[b].all(), ALU.add)

            if l == 0 and p == 0:
                dump("mg", View(mergedT.ap.rearrange("p a b -> p (a b)"), mergedT.all().regs))
                dump("xm", View(xT[:, :, 0:NP].ap, xT.all().regs), "p (a b) -> p a b", a=8)
            P.label = "L%dP%d ffn" % (l, p)
            rmsnorm(t0, SM_NFFN, xnT)
            for g in range(11):
                slot = next_w("wgu", l, g)
                if g == 6:
                    nxt = (l, p + 1) if p + 1 < NPASS else (l + 1, 0)
                    if nxt[0] < nl:
                        P.label = "L%dP%d norm1" % nxt
                        rms_stats(nxt[1] * NP, rstd2, split=True)
                        hoisted[0] = True
                        P.label = "L%dP%d ffn" % (l, p)
                    else:
                        hoisted[0] = False
                for i in range(2):
                    f = g * 2 + i
                    bgt = proj_fm(slot, 2 * i, xnT)
                    bu = proj_fm(slot, 2 * i + 1, xnT)
                    r = f % 2
                    act(sl[:, r, :], banks[bgt].all(), AF.Silu)
                    tt(gT[:, f, :], banks[bu].all(), sl[:, r, :], ALU.mult)
            for g in range(4):
                slot = next_w("wdn", l, g)
                for jj in range(2):
                    j = g * 2 + jj
                    b = nb()
                    for f in range(22):
                        mm(banks[b].all(), slot[:, f, jj * 128:(jj + 1) * 128], gT[:, f, :], f == 0, f == 21)
                    tt(xT[:, j, t0:t0 + NP], xT[:, j, t0:t0 + NP], banks[b].all(), ALU.add)
            if l == 0 and p == 0:
                dump("x1", View(xT[:, :, 0:NP].ap, xT.all().regs), "p (a b) -> p a b", a=8)

    for p in range(NPASS):
        t0 = p * NP
        if final_norm:
            for kc in range(8):
                act(sq[:, kc, :], xT[:, kc, t0:t0 + NP], AF.Square)
            b = nb()
            for kc in range(8):
                mm(banks[b].all(), onesb.all(), sq[:, kc, :], kc == 0, kc == 7)
            ts(rstd.all(), banks[b].all(), 1.0 / D, ALU.mult, EPS, ALU.add)
            rsqrt_big(rstd.all(), rstd.all())
        for tq in range(4):
            ti = p * 4 + tq
            r = ti % 2
            for kc in range(8):
                src = xT[:, kc, ti * 128:(ti + 1) * 128]
                if final_norm:
                    stt(xo[:, r, kc, :], src, nfin[:, kc:kc + 1], rstd[:, tq * 128:(tq + 1) * 128], ALU.mult, ALU.mult)
                else:
                    cp("dve", xo[:, r, kc, :], src)
            for half in range(2):
                b = nb()
                for q in range(4):
                    kc = half * 4 + q
                    tr(banks[b][:, q * 128:(q + 1) * 128], xo[:, r, kc, :], ident.all())
                cp("act" if half == 0 else "dve", yout[:, r, half * 512:(half + 1) * 512], banks[b].all())
            dma_out(y_d[ti * 128:(ti + 1) * 128, :], yout[:, r, :], "yout%d" % r)

    fk = [k for k in P.dma_cnt if k.startswith("yout") or k.startswith("dbg_")]
    nw = emit_all(P, nc, final_keys=fk)
    st.close()
    global _LAST_PROG
    _LAST_PROG = P
    print("ops:", len(P.ops), "waits:", nw, {e: P.seq[e] for e in P.ENGS})
    return nc


_PROG_CACHE = {}
_LAST_PROG = None


def _run(x, lay, nl, final_norm, dbg_names=()):
    key = (nl, final_norm, tuple(dbg_names))
    nc = build_program(nl, final_norm, dbg_names)
    in_maps = []
    for b in range(8):
        m = {"x": np.ascontiguousarray(x[b])}
        for k in ("win", "wbr", "wout", "wgu", "wdn", "small", "sgw", "sgb", "consts", "rope"):
            m[k] = lay[k]
        in_maps.append(m)
    res = run_bass_kernel_spmd(nc, in_maps, core_ids=list(range(8)))
    return res


def kernel(**inputs):
    inp = {k: np.asarray(v, dtype=np.float32) for k, v in inputs.items()}
    x = inp["x"]
    lay = _host_layout(inp, [0, 1])
    res = _run(x, lay, 2, True)
    out = np.stack([res.results[b]["y"] for b in range(8)], axis=0)
    return out.astype(np.float32)
```

```python
import numpy as np
import concourse.bass as bass
import concourse.mybir as mybir

F32 = mybir.dt.float32
BF16 = mybir.dt.bfloat16
AF = mybir.ActivationFunctionType
ALU = mybir.AluOpType
AX = mybir.AxisListType
_DSZ = {F32: 4, BF16: 2}


class View:
    __slots__ = ("ap", "regs")

    def __init__(self, ap, regs):
        self.ap = ap
        self.regs = regs


class Buf:
    def __init__(self, key, base_ap, off_bytes, shape, dtype):
        self.key = key
        self.shape = tuple(shape)
        self.dtype = dtype
        self.esz = _DSZ[dtype]
        self.off = off_bytes
        n = int(np.prod(shape))
        self.nbytes = n * self.esz
        a = base_ap[:, off_bytes // 4:(off_bytes + self.nbytes) // 4]
        if dtype != F32:
            a = a.bitcast(dtype)
        if len(shape) > 1:
            names = " ".join("d%d" % i for i in range(len(shape)))
            kw = {"d%d" % i: int(s) for i, s in enumerate(shape)}
            a = a.rearrange("p (%s) -> p %s" % (names, names), **kw)
        self.ap = a
        st = [1] * len(shape)
        for i in range(len(shape) - 2, -1, -1):
            st[i] = st[i + 1] * shape[i + 1]
        self.strides = st

    def __getitem__(self, key):
        if not isinstance(key, tuple):
            key = (key,)
        pk = key[0]
        fk = list(key[1:]) + [slice(None)] * (len(self.shape) - (len(key) - 1))
        ap = self.ap[(pk,) + tuple(fk)]
        rng = []
        for d, k in enumerate(fk):
            if isinstance(k, int):
                rng.append((k, k + 1))
            else:
                lo = 0 if k.start is None else k.start
                hi = self.shape[d] if k.stop is None else k.stop
                assert k.step in (None, 1)
                assert 0 <= lo < hi <= self.shape[d], (self.key, key, self.shape)
                rng.append((lo, hi))
        nd = len(rng)
        t = nd - 1
        while t > 0 and rng[t] == (0, self.shape[t]):
            t -= 1
        outer = [range(lo, hi) for (lo, hi) in rng[:t]]
        cnt = int(np.prod([len(r) for r in outer])) if outer else 1
        regs = []
        if cnt > 64:
            lo = sum(r[0] * s for r, s in zip(rng, self.strides))
            hi = sum((r[1] - 1) * s for r, s in zip(rng, self.strides)) + 1
            regs.append((self.key, self.off + lo * self.esz, self.off + hi * self.esz))
        else:
            import itertools
            for idx in itertools.product(*outer) if outer else [()]:
                base = sum(i * s for i, s in zip(idx, self.strides[:t]))
                lo = base + rng[t][0] * self.strides[t]
                hi = base + rng[t][1] * self.strides[t]
                regs.append((self.key, self.off + lo * self.esz, self.off + hi * self.esz))
        return View(ap, regs)

    def all(self):
        return self[:]


class Arena:
    def __init__(self, key, base_ap, nbytes):
        self.key = key
        self.base = base_ap
        self.nbytes = nbytes
        self.cur = 0

    def alloc(self, shape, dtype, at=None):
        n = int(np.prod(shape)) * _DSZ[dtype]
        n4 = (n + 3) // 4 * 4
        if at is None:
            at = self.cur
            self.cur += n4
        assert at % 4 == 0
        assert at + n4 <= self.nbytes, ("arena overflow", self.key, at + n4, self.nbytes)
        return Buf(self.key, self.base, at, shape, dtype)


class _Op:
    __slots__ = ("eng", "fn", "waits", "seq", "sig", "dma", "dcnt", "clock")


class Prog:
    ENGS = ("pe", "act", "dve", "pool", "sp")

    def __init__(self, nc):
        self.nc = nc
        self.ops = []
        self.hist = {}
        self.seq = {e: 0 for e in self.ENGS}
        self.last_clock = {e: {} for e in self.ENGS}
        self.dma_cnt = {}
        self.eng_ops = {e: [] for e in self.ENGS}
        self.label = ""
        self.labels = []

    def _deps(self, reads, writes):
        deps = []
        for (key, lo, hi) in reads:
            for r in self.hist.get(key, ()):
                if r[3] and r[0] < hi and lo < r[1]:
                    deps.append((r[2], True))
        for (key, lo, hi) in writes:
            for r in self.hist.get(key, ()):
                if r[0] < hi and lo < r[1]:
                    deps.append((r[2], False))
        return deps

    def _record(self, idx, reads, writes):
        for (key, lo, hi) in reads:
            self.hist.setdefault(key, []).append([lo, hi, idx, False])
        for (key, lo, hi) in writes:
            h = self.hist.setdefault(key, [])
            h[:] = [r for r in h if not (lo <= r[0] and r[1] <= hi)]
            h.append([lo, hi, idx, True])

    def add(self, eng, fn, reads=(), writes=(), dma=None):
        rr = []
        for v in reads:
            rr.extend(v.regs)
        ww = []
        for v in writes:
            ww.extend(v.regs)
        if eng == "pe":
            ww = [(k, lo // 2048 * 2048, -(-hi // 2048) * 2048) if k == "ps" else (k, lo, hi) for (k, lo, hi) in ww]
        op = _Op()
        op.eng = eng
        op.fn = fn
        op.dma = dma
        op.sig = False
        idx = len(self.ops)
        self.seq[eng] += 1
        op.seq = self.seq[eng]
        clock = dict(self.last_clock[eng])
        need = {}
        for (di, is_raw) in self._deps(rr, ww):
            d = self.ops[di]
            if d.dma is not None:
                tk = ("dma", d.dma)
                tv = d.dcnt
            else:
                if d.eng == eng:
                    if (op.seq - d.seq) <= 2 and eng != "pe":
                        tk = ("eng", eng)
                        tv = d.seq
                        if need.get(tk, 0) < tv:
                            need[tk] = tv
                    continue
                tk = ("eng", d.eng)
                tv = d.seq
            if clock.get(tk, 0) >= tv:
                continue
            if need.get(tk, 0) < tv:
                need[tk] = tv
            for k, v in d.clock.items():
                if clock.get(k, 0) < v:
                    clock[k] = v
            clock[tk] = max(clock.get(tk, 0), tv)
        op.waits = need
        self.last_clock[eng] = clock
        if dma is not None:
            self.dma_cnt[dma] = self.dma_cnt.get(dma, 0) + 1
            op.dcnt = self.dma_cnt[dma]
            c2 = dict(clock)
            c2[("eng", eng)] = max(c2.get(("eng", eng), 0), op.seq - 1)
            op.clock = c2
        else:
            op.dcnt = 0
            c2 = dict(clock)
            c2[("eng", eng)] = op.seq
            op.clock = c2
        self.ops.append(op)
        self.labels.append(self.label)
        self.eng_ops[eng].append(idx)
        self._record(idx, rr, ww)
        return idx

    def emit(self, sems, dma_sems):
        nc = self.nc
        engobj = {"pe": nc.tensor, "act": nc.scalar, "dve": nc.vector, "pool": nc.gpsimd, "sp": nc.sync}
        for op in self.ops:
            for (kind, key), v in op.waits.items():
                if kind == "eng":
                    self.ops[self.eng_ops[key][v - 1]].sig = True
        sigcount = {}
        for e in self.ENGS:
            c = 0
            arr = []
            for oi in self.eng_ops[e]:
                if self.ops[oi].sig:
                    c += 1
                arr.append(c)
            sigcount[e] = arr
        nw = 0
        for op in self.ops:
            eo = engobj[op.eng]
            for (kind, key), v in op.waits.items():
                if kind == "eng":
                    eo.wait_ge(sems[key], sigcount[key][v - 1])
                else:
                    eo.wait_ge(dma_sems[key], 16 * v)
                nw += 1
            ins = op.fn()
            if op.dma is not None:
                ins.then_inc(dma_sems[op.dma], 16)
            elif op.sig:
                ins.then_inc(sems[op.eng], 1)
        return nw


def emit_all(P, nc, final_keys=()):
    import contextlib
    with contextlib.ExitStack() as st:
        sems = {e: st.enter_context(nc.semaphore("s_" + e)) for e in P.ENGS}
        dsems = {k: st.enter_context(nc.semaphore("d_" + str(k))) for k in P.dma_cnt}
        nw = P.emit(sems, dsems)
        for k in final_keys:
            nc.sync.wait_ge(dsems[k], 16 * P.dma_cnt[k])
    return nw


import itertools
from concourse.bass_utils import run_bass_kernel_spmd

D = 1024
S = 2048
NP = 512
NPASS = S // NP
DFF = 2816
EPS = 1e-6
PW = 8448

SM_NMIX, SM_NFFN, SM_SGG, SM_SGB, SM_CVB, SM_CVG, SM_CVLB, SM_SINK, SM_SCW, SM_CVW, SM_N = 0, 8, 16, 20, 24, 28, 32, 36, 44, 56, 180
C_ID, C_M01, C_AM, C_NF, C_ROT, C_N = 0, 128, 256, 512, 520, 648


def _win_perm():
    p = []
    p += list(range(512, 1024))
    p += list(range(0, 512))
    for c in range(4):
        p += list(range(1024 + c * 128, 1024 + (c + 1) * 128))
        p += list(range(1536 + c * 128, 1536 + (c + 1) * 128))
    for j in range(4):
        for hq in (j, 4 + j):
            p += [2048 + hq * 64 + d for d in range(64)]
    p += list(range(2560, 2688))
    p += list(range(2688, 2816))
    for c in range(4):
        p += list(range(3328 + c * 128, 3328 + (c + 1) * 128))
        p += list(range(3840 + c * 128, 3840 + (c + 1) * 128))
        p += list(range(2816 + c * 128, 2816 + (c + 1) * 128))
    for j in range(8):
        for n in range(4):
            p += list(range(4352 + n * 1024 + j * 128, 4352 + n * 1024 + (j + 1) * 128))
    assert len(p) == PW
    return np.array(p, dtype=np.int64)


WIN_GROUPS = [512] * 5 + [256] + [384] * 4 + [512] * 8
WIN_OFF = [int(x) for x in np.cumsum([0] + WIN_GROUPS[:-1])]


def _host_layout(inp, layers):
    f = np.float32
    perm = _win_perm()
    win = np.stack([np.ascontiguousarray(inp["w_in"][l][:, perm]) for l in layers]).astype(f)
    wbr = np.stack([np.ascontiguousarray(
        inp["w_branch"][l].transpose(1, 0, 2).reshape(512, 4, 8, 128).transpose(2, 0, 1, 3)).reshape(8, 512, 512)
        for l in layers]).astype(f)
    wout = np.stack([inp["w_out"][l] for l in layers]).astype(f)
    gu_perm = []
    for fch in range(22):
        gu_perm += list(range(fch * 128, (fch + 1) * 128))
        gu_perm += list(range(DFF + fch * 128, DFF + (fch + 1) * 128))
    gu_perm = np.array(gu_perm)
    wgu = np.stack([np.ascontiguousarray(inp["w_gate_up"][l][:, gu_perm]) for l in layers]).astype(f)
    wdn = np.stack([inp["w_down"][l] for l in layers]).astype(f)
    small = np.zeros((len(layers), 128, SM_N), f)
    sgw = np.zeros((len(layers), 128, 512), f)
    sgb = np.zeros((len(layers), 128, 512), f)
    for i, l in enumerate(layers):
        small[i, :, SM_NMIX:SM_NMIX + 8] = inp["norm_mix"][l].reshape(8, 128).T
        small[i, :, SM_NFFN:SM_NFFN + 8] = inp["norm_ffn"][l].reshape(8, 128).T
        small[i, :, SM_SGG:SM_SGG + 4] = inp["sg_ln_g"][l].reshape(4, 128).T
        small[i, :, SM_SGB:SM_SGB + 4] = inp["sg_ln_b"][l].reshape(4, 128).T
        small[i, :, SM_CVB:SM_CVB + 4] = inp["cv_b"][l].reshape(4, 128).T
        small[i, :, SM_CVG:SM_CVG + 4] = inp["cv_ln_g"][l].reshape(4, 128).T
        small[i, :, SM_CVLB:SM_CVLB + 4] = inp["cv_ln_b"][l].reshape(4, 128).T
        small[i, :, SM_SINK:SM_SINK + 8] = np.broadcast_to(inp["attn_sinks"][l][None, :], (128, 8))
        small[i, :, SM_SCW:SM_SCW + 12] = inp["sc_w"][l].reshape(3, 4, 128).transpose(2, 1, 0).reshape(128, 12)
        small[i, :, SM_CVW:SM_CVW + 124] = inp["cv_w"][l].reshape(31, 4, 128).transpose(2, 1, 0).reshape(128, 124)
        sgw[i] = inp["sg_w"][l].transpose(2, 0, 1).reshape(128, 512)
        sgb[i] = np.broadcast_to(inp["sg_b"][l].reshape(1, 512), (128, 512))
    consts = np.zeros((128, C_N), f)
    consts[:, C_ID:C_ID + 128] = np.eye(128, dtype=f)
    s_i = np.arange(128)[:, None]
    t_i = np.arange(128)[None, :]
    consts[:, C_M01:C_M01 + 128] = (s_i <= t_i).astype(f)
    am = np.full((128, 256), -30000.0, f)
    qi = np.arange(128)[:, None]
    kj = np.arange(256)[None, :]
    delta = qi + 128 - kj
    am[(delta >= 0) & (delta < 128)] = 0.0
    consts[:, C_AM:C_AM + 256] = am
    consts[:, C_NF:C_NF + 8] = inp["norm_final"].reshape(8, 128).T
    m_i = np.arange(128)
    rot_i = (m_i % 64 + 32) % 64 + 64 * (m_i // 64)
    rotm = np.zeros((128, 128), f)
    rotm[rot_i, m_i] = 1.0
    consts[:, C_ROT:C_ROT + 128] = rotm
    pos = np.arange(S, dtype=f)
    inv_freq = (1.0 / (np.float32(10000.0) ** (np.arange(0, 64, 2, dtype=f) / np.float32(64)))).astype(f)
    ang = (pos[:, None] * inv_freq[None, :]).astype(f)
    cos = np.cos(ang).astype(f)
    sin = np.sin(ang).astype(f)
    d = np.arange(128) % 64
    rope = np.zeros((2, 128, S), f)
    rope[0] = cos[:, d % 32].T
    sgn = np.where(d < 32, -1.0, 1.0).astype(f)
    rope[1] = sin[:, d % 32].T * sgn[:, None]
    return dict(win=win, wbr=wbr, wout=wout, wgu=wgu, wdn=wdn, small=small, sgw=sgw, sgb=sgb,
                consts=consts, rope=rope)


DIAG_ENG = "dve"


def build_program(nl, final_norm, dbg_names=()):
    nc = bass.Bass("TRN2", target_bir_lowering=False)

    def dram(name, shape, kind="ExternalInput"):
        return nc.dram_tensor(name, list(shape), F32, kind=kind).ap()

    x_d = dram("x", [S, D])
    y_d = dram("y", [S, D], "ExternalOutput")
    win_d = dram("win", [nl, D, PW])
    wbr_d = dram("wbr", [nl, 8, 512, 512])
    wout_d = dram("wout", [nl, D, D])
    wgu_d = dram("wgu", [nl, D, 2 * DFF])
    wdn_d = dram("wdn", [nl, DFF, D])
    small_d = dram("small", [nl, 128, SM_N])
    sgw_d = dram("sgw", [nl, 128, 512])
    sgb_d = dram("sgb", [nl, 128, 512])
    consts_d = dram("consts", [128, C_N])
    rope_d = dram("rope", [2, 128, S])
    diag_d = nc.dram_tensor("diag_scratch", [nl, 4, 128, 31 * 128], BF16, kind="Internal").ap()
    dbg_d = {n: nc.dram_tensor("dbg_" + n, [128, sz], (BF16 if bf else F32), kind="ExternalOutput").ap() for n, sz, bf in dbg_names}

    P = Prog(nc)
    E = P.add
    SBN = 206000
    import contextlib
    st = contextlib.ExitStack()
    sb_t = st.enter_context(nc.sbuf_tensor("arena", [128, SBN // 4], F32))
    ps_t = st.enter_context(nc.psum_tensor("psum", [128, 4096], F32))
    SB = Arena("sb", sb_t[:], SBN)
    PS = Arena("ps", ps_t[:], 16384)
    banks = [PS.alloc([512], F32) for _ in range(8)]
    banks_bf = [Buf("ps", ps_t[:], b * 2048, [1024], BF16) for b in range(8)]
    pairs = [Buf("ps", ps_t[:], b * 4096, [1024], F32) for b in range(4)]
    bstate = [0]
    bank_lo = [0]

    def nb():
        b = bstate[0] % 8
        if b < bank_lo[0]:
            bstate[0] += bank_lo[0] - b
            b = bank_lo[0]
        bstate[0] += 1
        return b

    def nb2():
        if bstate[0] % 2:
            bstate[0] += 1
        b = (bstate[0] % 8) // 2
        bstate[0] += 2
        return b

    xT = SB.alloc([8, S], F32)
    cst = SB.alloc([C_N], F32)
    identb = SB.alloc([128], BF16)
    onesb = SB.alloc([128], BF16)
    mhalf = SB.alloc([1], F32)
    amb = SB.alloc([256], BF16)
    rotb = SB.alloc([128], BF16)
    small = SB.alloc([SM_N], F32)
    nsink = SB.alloc([8], F32)
    wtril = SB.alloc([4, 128], BF16)
    T1 = SB.alloc([4, 128], F32)
    cosT = SB.alloc([NP], F32)
    sinT = SB.alloc([NP], F32)
    xnT = SB.alloc([8, NP], BF16)
    rstd = SB.alloc([NP], F32)
    rstd2 = SB.alloc([NP], F32)
    wslots = [SB.alloc([8, 512], BF16) for _ in range(3)]
    yglu = SB.alloc([4, 30 + NP], BF16)
    kT = [SB.alloc([128 + NP], BF16) for _ in range(2)]
    Vpad = SB.alloc([5, 2, 2, 128], BF16)
    prod = SB.alloc([4, 2 + NP], F32)
    X0 = SB.cur
    yT = SB.alloc([4, 4, NP], BF16)
    A0 = SB.cur
    diag = SB.alloc([2, 31, 128], BF16)
    ybf = SB.alloc([4, NP], BF16)
    ysq = SB.alloc([4, NP], BF16)
    sig = SB.alloc([2, NP], BF16)
    cmean = SB.alloc([NP], F32)
    crstd = SB.alloc([NP], F32)
    ctmp = SB.alloc([2, NP], F32)
    A1 = SB.cur
    SB.cur = A0
    cgs = SB.alloc([NP], F32)
    acc = SB.alloc([NP], F32)
    bgs = SB.alloc([2, NP], BF16)
    mergedT = SB.alloc([8, NP], BF16)
    sg = SB.alloc([4, NP], BF16)
    wbrs = [SB.alloc([4, 4, 128], BF16) for _ in range(2)]
    macc = SB.alloc([NP], F32)
    mtmp = SB.alloc([2, NP], F32)
    A2 = SB.cur
    SB.cur = max(A1, A2)
    B0 = SB.cur
    uT = SB.alloc([4, NP], BF16)
    vhat = SB.alloc([4, NP], BF16)
    vgt = SB.alloc([2, NP], F32)
    gst = SB.alloc([64], F32)
    gtmp = SB.alloc([NP], F32)
    B1 = SB.cur
    SB.cur = B0
    qT = SB.alloc([4, NP], BF16)
    rt1 = SB.alloc([NP], F32)
    rt2 = SB.alloc([NP], F32)
    pu = SB.alloc([2, 4, 256], BF16)
    pb = SB.alloc([2, 4, 256], BF16)
    pTs = SB.alloc([2, 1024], BF16)
    ast = SB.alloc([2, 32], F32)
    qbf = SB.alloc([2, NP], BF16, at=pu.off)
    B2 = SB.cur
    SB.cur = B0
    sq = SB.alloc([8, NP], BF16)
    B3 = SB.cur
    SB.cur = B0
    sgw32 = SB.alloc([512], F32)
    sgb32 = SB.alloc([512], F32)
    B4 = SB.cur
    SB.cur = max(B1, B2, B3, B4)
    X1 = SB.cur
    SB.cur = X0
    gT = SB.alloc([22, NP], BF16)
    wds = [SB.alloc([22, 256], BF16) for _ in range(2)]
    sl = SB.alloc([2, NP], F32)
    F1 = SB.cur
    SB.cur = X0
    xin = SB.alloc([4, D], F32)
    xo = SB.alloc([2, 8, 128], F32)
    yout = SB.alloc([2, D], F32)
    F2 = SB.cur
    SB.cur = max(X1, F1, F2)
    assert SB.cur <= SBN, SB.cur
    print("SBUF bytes used per partition:", SB.cur)

    ident = Buf("sb", sb_t[:], cst.off + C_ID * 4, [128], F32)
    m01 = Buf("sb", sb_t[:], cst.off + C_M01 * 4, [128], F32)
    amask = Buf("sb", sb_t[:], cst.off + C_AM * 4, [256], F32)
    nfin = Buf("sb", sb_t[:], cst.off + C_NF * 4, [8], F32)
    rot32 = Buf("sb", sb_t[:], cst.off + C_ROT * 4, [128], F32)

    def smv(off, n=1):
        return small[:, off:off + n]

    def bc(v, shape):
        return View(v.ap.to_broadcast(list(shape)), v.regs)

    def mm(out_v, lhsT_v, rhs_v, start, stop):
        E("pe", lambda: nc.tensor.matmul(out_v.ap, lhsT=lhsT_v.ap, rhs=rhs_v.ap, start=start, stop=stop),
          reads=[lhsT_v, rhs_v], writes=[out_v])

    def tr(out_v, in_v, id_v):
        E("pe", lambda: nc.tensor.transpose(out=out_v.ap, in_=in_v.ap, identity=id_v.ap),
          reads=[in_v, id_v], writes=[out_v])

    def act(out_v, in_v, func, bias=None, scale=None, accum=None):
        rd = [in_v]
        kw = {}
        if bias is not None:
            if isinstance(bias, View):
                rd.append(bias)
                kw["bias"] = bias.ap
            else:
                kw["bias"] = float(bias)
        if scale is not None:
            if isinstance(scale, View):
                rd.append(scale)
                kw["scale"] = scale.ap
            else:
                kw["scale"] = float(scale)
        wr = [out_v]
        if accum is not None:
            wr.append(accum)
            kw["accum_out"] = accum.ap
        E("act", lambda: nc.scalar.activation(out=out_v.ap, in_=in_v.ap, func=func, **kw), reads=rd, writes=wr)

    def tt(out_v, a_v, b_v, op, eng="dve"):
        eo = nc.vector if eng == "dve" else nc.gpsimd
        E(eng, lambda: eo.tensor_tensor(out=out_v.ap, in0=a_v.ap, in1=b_v.ap, op=op), reads=[a_v, b_v], writes=[out_v])

    def ts(out_v, in_v, s1, op0, s2=None, op1=None, eng="dve"):
        eo = nc.vector if eng == "dve" else nc.gpsimd
        rd = [in_v]
        a1 = s1
        a2 = s2
        if isinstance(s1, View):
            rd.append(s1)
            a1 = s1.ap
        if isinstance(s2, View):
            rd.append(s2)
            a2 = s2.ap
        if op1 is None:
            E(eng, lambda: eo.tensor_scalar(out=out_v.ap, in0=in_v.ap, scalar1=a1, scalar2=None, op0=op0), reads=rd, writes=[out_v])
        else:
            E(eng, lambda: eo.tensor_scalar(out=out_v.ap, in0=in_v.ap, scalar1=a1, scalar2=a2, op0=op0, op1=op1), reads=rd, writes=[out_v])

    def stt(out_v, in0_v, sc, in1_v, op0, op1):
        rd = [in0_v, in1_v]
        a = sc
        if isinstance(sc, View):
            rd.append(sc)
            a = sc.ap
        E("dve", lambda: nc.vector.scalar_tensor_tensor(out=out_v.ap, in0=in0_v.ap, scalar=a, in1=in1_v.ap, op0=op0, op1=op1),
          reads=rd, writes=[out_v])

    def cp(eng, out_v, in_v):
        if eng == "act":
            E("act", lambda: nc.scalar.copy(out=out_v.ap, in_=in_v.ap), reads=[in_v], writes=[out_v])
        elif eng == "dve":
            E("dve", lambda: nc.vector.tensor_copy(out=out_v.ap, in_=in_v.ap), reads=[in_v], writes=[out_v])
        else:
            E("pool", lambda: nc.gpsimd.tensor_copy(out=out_v.ap, in_=in_v.ap), reads=[in_v], writes=[out_v])

    def mset(eng, out_v, val):
        eo = {"dve": nc.vector, "pool": nc.gpsimd}[eng]
        E(eng, lambda: eo.memset(out_v.ap, val), writes=[out_v])

    def rsqrt_pool(out_v, in_v):
        n = in_v.ap.shape
        E("pool", lambda: nc.gpsimd.tensor_tensor(out=out_v.ap, in0=in_v.ap, in1=mhalf[:, 0:1].ap.to_broadcast(list(n)), op=ALU.pow),
          reads=[in_v, mhalf.all()], writes=[out_v])

    def rsqrt_big(out_v, in_v):
        act(out_v, in_v, AF.Sqrt)
        E("dve", lambda: nc.vector.reciprocal(out=out_v.ap, in_=out_v.ap), reads=[out_v], writes=[out_v])

    def dma_sp(out_v, in_ap, key, src_views=()):
        E("sp", lambda: nc.sync.dma_start(out=out_v.ap, in_=in_ap), reads=list(src_views), writes=[out_v], dma=key)

    def dma_out(out_ap, in_v, key):
        E("sp", lambda: nc.sync.dma_start(out=out_ap, in_=in_v.ap), reads=[in_v], dma=key)

    def dma_w(out_v, in_ap, key):
        E("pool", lambda: nc.gpsimd.dma_start(out=out_v.ap, in_=in_ap), writes=[out_v], dma=key)

    def dump(name, view, rearr=None, **kw):
        if name in dbg_d:
            dst = dbg_d[name]
            if rearr:
                dst = dst.rearrange(rearr, **kw)
            dma_out(dst, view, "dbg_" + name)

    sched = []
    for l in range(nl):
        for p in range(NPASS):
            for g in (0, 1, 4, 5, 2, 3, 6, 7, 8, 9):
                sched.append(("win", l, g))
            for j in range(8):
                sched.append(("win", l, 10 + j))
                sched.append(("wbr", l, j))
            for g in range(2):
                sched.append(("wout", l, g))
            for g in range(11):
                sched.append(("wgu", l, g))
            for g in range(4):
                sched.append(("wdn", l, g))
    loaded = {}

    w_use = [0]
    _cls = lambda kk: "w512" if kk in ("win", "wout", "wgu") else kk
    cls_index = []
    _cc = {"w512": 0, "wbr": 0, "wdn": 0}
    pending = {"w512": [], "wbr": [], "wdn": []}
    for si, (kk, _l, _g) in enumerate(sched):
        cls_index.append(_cc[_cls(kk)])
        _cc[_cls(kk)] += 1
        pending[_cls(kk)].append(si)
    retired = {"w512": 0, "wbr": 0, "wdn": 0}
    nslots = {"w512": 3, "wbr": 2, "wdn": 2}
    look = {"w512": 1000, "wbr": 4, "wdn": 5}
    issued = set()
    gate_ok = {"wbr": True}

    def issue_idx(i):
        kind, l, g = sched[i]
        if kind in ("win", "wout", "wgu"):
            s = cls_index[i] % 3
            slot = wslots[s]
            if kind == "win":
                w = WIN_GROUPS[g]
                src = win_d[l][:, WIN_OFF[g]:WIN_OFF[g] + w]
            elif kind == "wout":
                w = 512
                src = wout_d[l][:, g * 512:(g + 1) * 512]
            else:
                w = 512
                src = wgu_d[l][:, g * 512:(g + 1) * 512]
            dma_w(slot[:, :, 0:w], src.rearrange("(kc p) n -> p kc n", p=128), "w512_%d" % s)
        elif kind == "wbr":
            s = cls_index[i] % 2
            slot = wbrs[s]
            dma_w(View(slot.ap.rearrange("p kc n d -> p kc (n d)"), slot.all().regs),
                  wbr_d[l][g].rearrange("(kc p) n -> p kc n", p=128), "wbr_%d" % s)
        else:
            s = cls_index[i] % 2
            slot = wds[s]
            dma_w(slot.all(), wdn_d[l][:, g * 256:(g + 1) * 256].rearrange("(kc p) n -> p kc n", p=128), "wdn_%d" % s)
        loaded[i] = slot
        issued.add(i)

    def next_w(kind, l, g):
        i = w_use[0]
        assert sched[i] == (kind, l, g), (sched[i], kind, l, g)
        c = _cls(kind)
        retired[c] = cls_index[i]
        progress = True
        while progress:
            progress = False
            heads = [(pending[cc][0], cc) for cc in pending if pending[cc]]
            for j, cc in sorted(heads):
                if cls_index[j] < retired[cc] + nslots[cc] and j <= i + look[cc] and gate_ok.get(cc, True):
                    issue_idx(j)
                    pending[cc].pop(0)
                    progress = True
                    break
        assert i in issued
        w_use[0] += 1
        return loaded.pop(i)

    dma_sp(cst.all(), consts_d, "c0")
    cp("dve", identb.all(), ident.all())
    cp("dve", amb.all(), amask.all())
    cp("dve", rotb.all(), rot32.all())
    mset("dve", onesb.all(), 1.0)
    mset("dve", mhalf.all(), -0.5)
    mset("dve", Vpad.all(), 0.0)
    mset("dve", kT[0].all(), 0.0)
    mset("dve", kT[1].all(), 0.0)

    for ti in range(16):
        r = ti % 4
        dma_sp(xin[:, r, :], x_d[ti * 128:(ti + 1) * 128, :], "xin%d" % r)
        for half in range(2):
            b = nb()
            for q in range(4):
                kc = half * 4 + q
                tr(banks[b][:, q * 128:(q + 1) * 128], xin[:, r, kc * 128:(kc + 1) * 128], ident.all())
            src = View(banks[b].ap.rearrange("p (a b) -> p a b", a=4), banks[b].all().regs)
            dst = xT[:, half * 4:(half + 1) * 4, ti * 128:(ti + 1) * 128]
            cp("act" if half == 0 else "dve", dst, src)

    def rms_stats(t0, rs, split=False):
        for kc in range(8):
            if kc % 2 == 0 or not split:
                act(sq[:, kc, :], xT[:, kc, t0:t0 + NP], AF.Square)
            else:
                tt(sq[:, kc, :], xT[:, kc, t0:t0 + NP], xT[:, kc, t0:t0 + NP], ALU.mult)
        b = nb()
        for kc in range(8):
            mm(banks[b].all(), onesb.all(), sq[:, kc, :], kc == 0, kc == 7)
        ts(rs.all(), banks[b].all(), 1.0 / D, ALU.mult, EPS, ALU.add)
        rsqrt_big(rs.all(), rs.all())

    def rms_apply(t0, gcol, out_bf, rs):
        for kc in range(8):
            stt(out_bf[:, kc, :], xT[:, kc, t0:t0 + NP], smv(gcol + kc), rs.all(), ALU.mult, ALU.mult)

    def rmsnorm(t0, gcol, out_bf):
        rms_stats(t0, rstd)
        rms_apply(t0, gcol, out_bf, rstd)

    def proj_fm(slot, c, rhsT, nk=8):
        b = nb()
        for kc in range(nk):
            mm(banks[b].all(), slot[:, kc, c * 128:(c + 1) * 128], rhsT[:, kc, :], kc == 0, kc == nk - 1)
        return b

    hoisted = [False]
    for l in range(nl):
        dma_sp(small.all(), small_d[l], "small")
        dma_sp(sgw32.all(), sgw_d[l], "sgw")
        dma_sp(sgb32.all(), sgb_d[l], "sgb")
        ts(nsink.all(), smv(SM_SINK, 8), -1.0, ALU.mult)
        wtv = View(wtril.ap.rearrange("p g t -> p (g t)"), wtril.all().regs)
        sg4 = View(sgw32.ap.rearrange("p (g t) -> p g t", g=4), sgw32.all().regs)
        tt(wtril.all(), sg4, bc(View(m01.ap.rearrange("p (o t) -> p o t", o=1), m01.all().regs), [128, 4, 128]), ALU.mult)
        b = nb()
        mm(banks[b].all(), onesb.all(), wtv, True, True)
        for g in range(4):
            stt(T1[:, g, :], banks[b][:, g * 128:(g + 1) * 128], smv(SM_SGB + g), sgb32[:, g * 128:(g + 1) * 128], ALU.mult, ALU.add)

        for c in range(4):
            idb = View(identb.ap.rearrange("p (o t) -> p o t", o=1), identb.all().regs)
            cw = small[:, SM_CVW + c * 31:SM_CVW + (c + 1) * 31]
            cw3 = View(cw.ap.rearrange("p (k o) -> p k o", o=1), cw.regs)
            tt(diag[:, c % 2], bc(idb, [128, 31, 128]), bc(cw3, [128, 31, 128]), ALU.mult)
            dgv = View(diag[:, c % 2].ap.rearrange("p k t -> p (k t)"), diag[:, c % 2].regs)
            E("sp", (lambda l=l, c=c, dgv=dgv: nc.sync.dma_start(out=diag_d[l][c], in_=dgv.ap)),
              reads=[dgv], writes=[View(None, [("dg", (l * 4 + c) * 100, (l * 4 + c) * 100 + 100)])], dma="dgw%d" % (c % 2))

        for p in range(NPASS):
            t0 = p * NP
            first = (p == 0)
            dma_sp(cosT.all(), rope_d[0][:, t0:t0 + NP], "cos")
            dma_sp(sinT.all(), rope_d[1][:, t0:t0 + NP], "sin")
            if first:
                mset("dve", yglu[:, :, 0:30], 0.0)
                mset("dve", prod[:, :, 0:2], 0.0)
            else:
                cp("dve", yglu[:, :, 0:30], yglu[:, :, NP:NP + 30])
                cp("dve", prod[:, :, 0:2], prod[:, :, NP:NP + 2])
                cp("dve", kT[0][:, 0:128], kT[0][:, NP:NP + 128])
                cp("dve", kT[1][:, 0:128], kT[1][:, NP:NP + 128])
                cp("dve", Vpad[:, 0], Vpad[:, 4])

            P.label = "L%dP%d norm1" % (l, p)
            gate_ok["wbr"] = False
            if hoisted[0]:
                rms_apply(t0, SM_NMIX, xnT, rstd2)
            else:
                rmsnorm(t0, SM_NMIX, xnT)
            if l == 0 and p == 0:
                dump("xn", xnT.all())

            P.label = "L%dP%d gmlp" % (l, p)
            slot = next_w("win", l, 0)
            for ti in range(4):
                b = nb()
                for kc in range(8):
                    mm(banks[b].all(), xnT[:, kc, ti * 128:(ti + 1) * 128], slot[:, kc, :], kc == 0, kc == 7)
                r = ti % 2
                act(vgt[:, r, :], banks[b].all(), AF.Gelu)
                E("dve", (lambda r=r, ti=ti: nc.vector.bn_stats(out=gst[:, ti * 8:ti * 8 + 6].ap, in_=vgt[:, r, :].ap)),
                  reads=[vgt[:, r, :]], writes=[gst[:, ti * 8:ti * 8 + 6]])
                E("dve", (lambda ti=ti: nc.vector.bn_aggr(out=gst[:, 32 + ti * 4:32 + ti * 4 + 2].ap, in_=gst[:, ti * 8:ti * 8 + 6].ap)),
                  reads=[gst[:, ti * 8:ti * 8 + 6]], writes=[gst[:, 32 + ti * 4:32 + ti * 4 + 2]])
                ts(gst[:, 32 + ti * 4 + 2:32 + ti * 4 + 3], gst[:, 32 + ti * 4 + 1:32 + ti * 4 + 2], EPS, ALU.add)
                rsqrt_pool(gst[:, 32 + ti * 4 + 3:32 + ti * 4 + 4], gst[:, 32 + ti * 4 + 2:32 + ti * 4 + 3])
                ts(vhat[:, ti, :], vgt[:, r, :], gst[:, 32 + ti * 4:32 + ti * 4 + 1], ALU.subtract,
                   gst[:, 32 + ti * 4 + 3:32 + ti * 4 + 4], ALU.mult)
            slot = next_w("win", l, 1)
            for c in range(4):
                b = proj_fm(slot, c, xnT)
                act(uT[:, c, :], banks[b].all(), AF.Gelu)
            for g in range(4):
                b = nb()
                for nch in range(4):
                    mm(banks[b][:, nch * 128:(nch + 1) * 128], vhat[:, nch, g * 128:(g + 1) * 128], wtril[:, g, :], True, True)
                src = View(banks[b].ap.rearrange("p (a b) -> p a b", a=4), banks[b].all().regs)
                t1b = bc(View(T1[:, g:g + 1, :].ap, T1[:, g, :].regs), [128, 4, 128])
                gt3 = View(gtmp.ap.rearrange("p (a b) -> p a b", a=4), gtmp.all().regs)
                E("dve", (lambda src=src, t1b=t1b, gt3=gt3, g=g: nc.vector.scalar_tensor_tensor(
                    out=gt3.ap, in0=src.ap, scalar=smv(SM_SGG + g).ap, in1=t1b.ap, op0=ALU.mult, op1=ALU.add)),
                  reads=[src, t1b, smv(SM_SGG + g)], writes=[gt3])
                tt(yT[:, 0, g, :], gtmp.all(), uT[:, g, :], ALU.mult)

            def build_diag(c):
                dgv = View(diag[:, c % 2].ap.rearrange("p k t -> p (k t)"), diag[:, c % 2].regs)
                E("sp", (lambda c=c, dgv=dgv, l=l: nc.sync.dma_start(out=dgv.ap, in_=diag_d[l][c])),
                  reads=[View(None, [("dg", (l * 4 + c) * 100, (l * 4 + c) * 100 + 100)])], writes=[dgv], dma="dgr%d" % (c % 2))

            build_diag(0)
            build_diag(1)
            P.label = "L%dP%d attn" % (l, p)
            def rot_half(bank_q, r):
                cp("act", qbf[:, r, :], banks[bank_q].all())
                bpp = nb()
                mm(banks[bpp].all(), rotb.all(), qbf[:, r, :], True, True)
                return bpp

            slot = next_w("win", l, 4)
            for j in range(4):
                bq = proj_fm(slot, j, xnT)
                bp = rot_half(bq, j % 2)
                E("dve", (lambda bq=bq: nc.vector.tensor_tensor(out=rt1.ap, in0=banks[bq].ap, in1=cosT.ap, op=ALU.mult)),
                  reads=[banks[bq].all(), cosT.all(), qbf[:, j % 2, :]], writes=[rt1.all()])
                tt(rt2.all(), banks[bp].all(), sinT.all(), ALU.mult)
                tt(qT[:, j, :], rt1.all(), rt2.all(), ALU.add)
            slot = next_w("win", l, 5)
            bk = proj_fm(slot, 0, xnT)
            bkp = rot_half(bk, 0)
            E("dve", (lambda bk=bk: nc.vector.tensor_tensor(out=rt1.ap, in0=banks[bk].ap, in1=cosT.ap, op=ALU.mult)),
              reads=[banks[bk].all(), cosT.all(), qbf[:, 0, :]], writes=[rt1.all()])
            tt(rt2.all(), banks[bkp].all(), sinT.all(), ALU.mult)
            tt(kT[0][0:64, 128:128 + NP], rt1[0:64, :], rt2[0:64, :], ALU.add)
            tt(kT[1][64:128, 128:128 + NP], rt1[64:128, :], rt2[64:128, :], ALU.add)
            bv = nb()
            for ti in range(4):
                for kc in range(8):
                    mm(banks[bv][:, ti * 128:(ti + 1) * 128], xnT[:, kc, ti * 128:(ti + 1) * 128], slot[:, kc, 128:256], kc == 0, kc == 7)
            for ti in range(4):
                srcv = View(banks[bv][:, ti * 128:(ti + 1) * 128].ap.rearrange("p (h d) -> p h d", h=2), banks[bv][:, ti * 128:(ti + 1) * 128].regs)
                cp("act", Vpad[:, 1 + ti, :, 0, 0:64], srcv)
                cp("act", Vpad[:, 1 + ti, :, 1, 64:128], srcv)

            def conf_units():
                for gi in range(2):
                    slot = next_w("win", l, 2 + gi)
                    for ci in range(2):
                        c = gi * 2 + ci
                        ba = proj_fm(slot, 2 * ci, xnT)
                        bg_ = proj_fm(slot, 2 * ci + 1, xnT)
                        r = c % 2
                        act(sig[:, r, :], banks[bg_].all(), AF.Sigmoid)
                        tt(yglu[:, c, 30:30 + NP], banks[ba].all(), sig[:, r, :], ALU.mult)
                        yield
                for c in range(4):
                    r = c % 2
                    b = nb()
                    for k in range(31):
                        mm(banks[b].all(), diag[:, r, k, :], yglu[:, c, k:k + NP], k == 0, k == 30)
                    if c + 2 < 4:
                        build_diag(c + 2)
                    act(ybf[:, c, :], banks[b].all(), AF.Identity, bias=smv(SM_CVB + c))
                    act(ysq[:, c, :], banks[b].all(), AF.Square, bias=smv(SM_CVB + c))
                    yield
                b1 = nb()
                for c in range(4):
                    mm(banks[b1].all(), onesb.all(), ybf[:, c, :], c == 0, c == 3)
                b2 = nb()
                for c in range(4):
                    mm(banks[b2].all(), onesb.all(), ysq[:, c, :], c == 0, c == 3)
                ts(cmean.all(), banks[b1].all(), 1.0 / 512, ALU.mult)
                tt(ctmp[:, 0, :], cmean.all(), cmean.all(), ALU.mult)
                stt(crstd.all(), banks[b2].all(), 1.0 / 512, ctmp[:, 0, :], ALU.mult, ALU.subtract)
                ts(crstd.all(), crstd.all(), EPS, ALU.add)
                yield

            def conf_tail_rsqrt():
                rsqrt_big(crstd.all(), crstd.all())

            def conf_tail(cs):
                for c in cs:
                    r = c % 2
                    tt(ctmp[:, r, :], ybf[:, c, :], cmean.all(), ALU.subtract)
                    tt(ctmp[:, r, :], ctmp[:, r, :], crstd.all(), ALU.mult)
                    act(yT[:, 1, c, :], ctmp[:, r, :], AF.Silu, bias=smv(SM_CVLB + c), scale=smv(SM_CVG + c))


            def sconv_units():
                for c in range(4):
                    slot = next_w("win", l, 6 + c)
                    bcg = proj_fm(slot, 0, xnT)
                    bh = proj_fm(slot, 1, xnT)
                    bbg = proj_fm(slot, 2, xnT)
                    r = c % 2
                    cp("act", cgs.all(), banks[bcg].all())
                    tt(prod[:, c, 2:2 + NP], banks[bh].all(), cgs.all(), ALU.mult)
                    cp("act", bgs[:, r, :], banks[bbg].all())
                    ts(acc.all(), prod[:, c, 2:2 + NP], smv(SM_SCW + c * 3 + 2), ALU.mult)
                    stt(acc.all(), prod[:, c, 1:1 + NP], smv(SM_SCW + c * 3 + 1), acc.all(), ALU.mult, ALU.add)
                    stt(acc.all(), prod[:, c, 0:NP], smv(SM_SCW + c * 3 + 0), acc.all(), ALU.mult, ALU.add)
                    tt(yT[:, 3, c, :], acc.all(), bgs[:, r, :], ALU.mult)
                    yield

            tstate = {}

            def att_A(t):
                n, h = t // 2, t % 2
                seq_first = (first and n == 0)
                nkb = 1 if seq_first else 2
                nk = nkb * 128
                k0 = 128 if seq_first else n * 128
                amv = amb[:, 256 - nk:256]
                sc = pairs[h]
                for g in range(4):
                    mm(sc[:, g * nk:(g + 1) * nk], qT[:, g, n * 128:(n + 1) * 128], kT[h][:, k0:k0 + nk], True, False)
                    mm(sc[:, g * nk:(g + 1) * nk], identb.all(), amv, False, True)
                sc3 = View(sc[:, 0:4 * nk].ap.rearrange("p (g k) -> p g k", g=4), sc[:, 0:4 * nk].regs)
                E("dve", (lambda: nc.vector.tensor_reduce(out=ast[:, h, 0:4].ap, in_=sc3.ap, axis=AX.X, op=ALU.max)),
                  reads=[sc3], writes=[ast[:, h, 0:4]])
                stt(ast[:, h, 4:8], ast[:, h, 0:4], -0.125, nsink[:, h * 4:(h + 1) * 4], ALU.mult, ALU.min)
                tt(ast[:, h, 12:16], ast[:, h, 4:8], smv(SM_SINK + h * 4, 4), ALU.add)
                pu3 = View(pu[:, h].ap.rearrange("p g k -> p (g k)")[:, 0:4 * nk].rearrange("p (g k) -> p g k", g=4), pu[:, h].regs)
                for g in range(4):
                    act(View(pu3.ap[:, g, :], pu[:, h].regs), View(sc3.ap[:, g, :], sc3.regs), AF.Exp,
                        bias=ast[:, h, 4 + g:5 + g], scale=0.125, accum=ast[:, h, 8 + g:9 + g])
                act(ast[:, h, 12:16], ast[:, h, 12:16], AF.Exp)
                tt(ast[:, h, 16:20], ast[:, h, 8:12], ast[:, h, 12:16], ALU.add)
                E("dve", (lambda: nc.vector.reciprocal(out=ast[:, h, 20:24].ap, in_=ast[:, h, 16:20].ap)),
                  reads=[ast[:, h, 16:20]], writes=[ast[:, h, 20:24]])
                pb3 = View(pb[:, h].ap.rearrange("p g k -> p (g k)")[:, 0:4 * nk].rearrange("p (g k) -> p g k", g=4), pb[:, h].regs)
                rd3 = View(ast[:, h, 20:24].ap.rearrange("p (g o) -> p g o", o=1).to_broadcast([128, 4, nk]), ast[:, h, 20:24].regs)
                tt(pb3, pu3, rd3, ALU.mult)
                tstate[t] = (n, h, seq_first, nkb, nk, pb3)

            def att_B(t):
                n, h, seq_first, nkb, nk, pb3 = tstate[t]
                bt = nb()
                ptb = banks_bf[bt]
                for g in range(4):
                    for kb in range(nkb):
                        tr(ptb[:, (g * nkb + kb) * 128:(g * nkb + kb + 1) * 128],
                           View(pb3.ap[:, g, kb * 128:(kb + 1) * 128], pb[:, h].regs), identb.all())
                cp("act", pTs[:, h, 0:4 * nk], ptb[:, 0:4 * nk])

            def att_C(t):
                n, h, seq_first, nkb, nk, pb3 = tstate[t]
                bo = nb()
                for prr in range(2):
                    cnt = 0
                    tot = 2 * nkb
                    for gi2 in range(2):
                        g = prr * 2 + gi2
                        for kb in range(nkb):
                            vt = (n + kb) if not seq_first else 1
                            mm(banks[bo][:, prr * 128:(prr + 1) * 128], Vpad[:, vt, h, gi2, :],
                               pTs[:, h, (g * nkb + kb) * 128:(g * nkb + kb + 1) * 128], cnt == 0, cnt == tot - 1)
                            cnt += 1
                srcv = View(banks[bo][:, 0:256].ap.rearrange("p (a b) -> p a b", a=2), banks[bo][:, 0:256].regs)
                cp("act", yT[:, 2, h * 2:h * 2 + 2, n * 128:(n + 1) * 128], srcv)

            bank_lo[0] = 4
            og = itertools.chain(conf_units(), sconv_units())

            def other(k=1):
                for _ in range(k):
                    P.label = "L%dP%d conf" % (l, p)
                    try:
                        next(og)
                    except StopIteration:
                        pass
                P.label = "L%dP%d attn" % (l, p)

            P.label = "L%dP%d attn" % (l, p)
            att_A(0)
            other(1)
            att_A(1)
            other(1)
            for t in range(8):
                att_B(t)
                other(1)
                att_C(t)
                if t + 2 < 8:
                    att_A(t + 2)
                if t <= 3:
                    other(1)
                if t == 5:
                    P.label = "L%dP%d conf" % (l, p)
                    conf_tail_rsqrt()
                    P.label = "L%dP%d attn" % (l, p)
                if t == 6:
                    P.label = "L%dP%d conf" % (l, p)
                    conf_tail((0, 1))
                    P.label = "L%dP%d attn" % (l, p)
            other(40)
            P.label = "L%dP%d conf" % (l, p)
            conf_tail((2, 3))
            bank_lo[0] = 0

            if l == 0 and p == 0:
                dump("yT", View(yT.ap.rearrange("p a b c -> p (a b c)"), yT.all().regs))

            P.label = "L%dP%d merge" % (l, p)
            gate_ok["wbr"] = True
            for j in range(8):
                slot = next_w("win", l, 10 + j)
                wb = next_w("wbr", l, j)
                for ni, n in enumerate((0, 3, 2, 1)):
                    bgt = proj_fm(slot, n, xnT)
                    bb = nb()
                    for kc in range(4):
                        mm(banks[bb].all(), wb[:, kc, n, :], yT[:, n, kc, :], kc == 0, kc == 3)
                    act(sg[:, n, :], banks[bgt].all(), AF.Sigmoid)
                    if ni == 0:
                        tt(macc.all(), banks[bb].all(), sg[:, n, :], ALU.mult)
                    else:
                        r = ni % 2
                        tt(mtmp[:, r, :], banks[bb].all(), sg[:, n, :], ALU.mult)
                        if ni < 3:
                            tt(macc.all(), macc.all(), mtmp[:, r, :], ALU.add)
                        else:
                            tt(mergedT[:, j, :], macc.all(), mtmp[:, r, :], ALU.add)
            P.label = "L%dP%d wout" % (l, p)
            for g in range(2):
                slot = next_w("wout", l, g)
                for jj in range(4):
                    j = g * 4 + jj
                    b = proj_fm(slot, jj, mergedT)
                    tt(xT[:, j, t0:t0 + NP], xT[:, j, t0:t0 + NP], banks[b].all(), ALU.add)

            if l == 0 and p == 0:
                dump("mg", View(mergedT.ap.rearrange("p a b -> p (a b)"), mergedT.all().regs))
                dump("xm", View(xT[:, :, 0:NP].ap, xT.all().regs), "p (a b) -> p a b", a=8)
            P.label = "L%dP%d ffn" % (l, p)
            rmsnorm(t0, SM_NFFN, xnT)
            for g in range(11):
                slot = next_w("wgu", l, g)
                if g == 6:
                    nxt = (l, p + 1) if p + 1 < NPASS else (l + 1, 0)
                    if nxt[0] < nl:
                        P.label = "L%dP%d norm1" % nxt
                        rms_stats(nxt[1] * NP, rstd2, split=True)
                        hoisted[0] = True
                        P.label = "L%dP%d ffn" % (l, p)
                    else:
                        hoisted[0] = False
                for i in range(2):
                    f = g * 2 + i
                    bgt = proj_fm(slot, 2 * i, xnT)
                    bu = proj_fm(slot, 2 * i + 1, xnT)
                    r = f % 2
                    act(sl[:, r, :], banks[bgt].all(), AF.Silu)
                    tt(gT[:, f, :], banks[bu].all(), sl[:, r, :], ALU.mult)
            for g in range(4):
                slot = next_w("wdn", l, g)
                for jj in range(2):
                    j = g * 2 + jj
                    b = nb()
                    for f in range(22):
                        mm(banks[b].all(), slot[:, f, jj * 128:(jj + 1) * 128], gT[:, f, :], f == 0, f == 21)
                    tt(xT[:, j, t0:t0 + NP], xT[:, j, t0:t0 + NP], banks[b].all(), ALU.add)
            if l == 0 and p == 0:
                dump("x1", View(xT[:, :, 0:NP].ap, xT.all().regs), "p (a b) -> p a b", a=8)

    for p in range(NPASS):
        t0 = p * NP
        if final_norm:
            for kc in range(8):
                act(sq[:, kc, :], xT[:, kc, t0:t0 + NP], AF.Square)
            b = nb()
            for kc in range(8):
                mm(banks[b].all(), onesb.all(), sq[:, kc, :], kc == 0, kc == 7)
            ts(rstd.all(), banks[b].all(), 1.0 / D, ALU.mult, EPS, ALU.add)
            rsqrt_big(rstd.all(), rstd.all())
        for tq in range(4):
            ti = p * 4 + tq
            r = ti % 2
            for kc in range(8):
                src = xT[:, kc, ti * 128:(ti + 1) * 128]
                if final_norm:
                    stt(xo[:, r, kc, :], src, nfin[:, kc:kc + 1], rstd[:, tq * 128:(tq + 1) * 128], ALU.mult, ALU.mult)
                else:
                    cp("dve", xo[:, r, kc, :], src)
            for half in range(2):
                b = nb()
                for q in range(4):
                    kc = half * 4 + q
                    tr(banks[b][:, q * 128:(q + 1) * 128], xo[:, r, kc, :], ident.all())
                cp("act" if half == 0 else "dve", yout[:, r, half * 512:(half + 1) * 512], banks[b].all())
            dma_out(y_d[ti * 128:(ti + 1) * 128, :], yout[:, r, :], "yout%d" % r)

    fk = [k for k in P.dma_cnt if k.startswith("yout") or k.startswith("dbg_")]
    nw = emit_all(P, nc, final_keys=fk)
    st.close()
    global _LAST_PROG
    _LAST_PROG = P
    print("ops:", len(P.ops), "waits:", nw, {e: P.seq[e] for e in P.ENGS})
    return nc


_PROG_CACHE = {}
_LAST_PROG = None


def _run(x, lay, nl, final_norm, dbg_names=()):
    key = (nl, final_norm, tuple(dbg_names))
    nc = build_program(nl, final_norm, dbg_names)
    in_maps = []
    for b in range(8):
        m = {"x": np.ascontiguousarray(x[b])}
        for k in ("win", "wbr", "wout", "wgu", "wdn", "small", "sgw", "sgb", "consts", "rope"):
            m[k] = lay[k]
        in_maps.append(m)
    res = run_bass_kernel_spmd(nc, in_maps, core_ids=list(range(8)))
    return res


def kernel(**inputs):
    inp = {k: np.asarray(v, dtype=np.float32) for k, v in inputs.items()}
    x = inp["x"]
    lay = _host_layout(inp, [0, 1])
    res = _run(x, lay, 2, True)
    out = np.stack([res.results[b]["y"] for b in range(8)], axis=0)
    return out.astype(np.float32)
```
